# Optimizing a Trainium2 kernel written in Bass

```python
import jax, jax.numpy as jnp
from jax import lax
import numpy as np

D_MODEL = 1024
BATCH = 2
SEQ = 8192
DEPTH = 2

CHUNK = 64
SUB_CHUNK = 16
POOL_WIDTH = 512
POOL_GROUPS = 4
POOL_WINDOWS = (2, 4, 8, 16)
POOL_GROUP_DIM = POOL_WIDTH // POOL_GROUPS
HGRN_HEADS = 4
HGRN_EXPAND = 128
HGRN_HEAD_DIM = 128
HGRN_FORGET = HGRN_HEADS * HGRN_EXPAND
HGRN_INPUT = HGRN_HEADS * HGRN_HEAD_DIM
N_BRANCHES = 2
D_IN = POOL_WIDTH + 2 * HGRN_FORGET + 2 * HGRN_INPUT + N_BRANCHES * D_MODEL
D_FF = 2816
EPS = 1e-6

kernel_name = 'hybrid_pool_hgrn2_macaron_encoder'


def rmsnorm(x, g):
    xf = x.astype(jnp.float32)
    y = xf * lax.rsqrt(jnp.mean(xf * xf, axis=-1, keepdims=True) + EPS)
    return (y * g.astype(jnp.float32)).astype(x.dtype)


def swiglu(x, w_gate, w_up, w_down):
    return (jax.nn.silu(x @ w_gate) * (x @ w_up)) @ w_down


def causal_pool_mixer(u, pool_w, pool_scale):
    b_, s_, _ = u.shape
    ug = u.reshape(b_, s_, POOL_GROUPS, POOL_GROUP_DIM)
    cs = jnp.cumsum(ug.astype(jnp.float32), axis=1)
    cs = jnp.pad(cs, ((0, 0), (1, 0), (0, 0), (0, 0)))
    pos = jnp.arange(1, s_ + 1)
    means = []
    for g, w in enumerate(POOL_WINDOWS):
        lo = jnp.maximum(pos - w, 0)
        window_sum = cs[:, 1:, g] - cs[:, lo, g]
        count = (pos - lo).astype(jnp.float32)
        means.append(window_sum / count[None, :, None])
    pooled = jnp.stack(means, axis=2).astype(u.dtype)
    mixed = jnp.einsum('bsgc,gcd->bsgd', pooled - ug, pool_w)
    return mixed.reshape(b_, s_, POOL_WIDTH) * pool_scale


def hgrn2_chunkwise(q, k, v, log_f):
    b_, h_, s_, dk = q.shape
    dv = v.shape[-1]
    n = s_ // CHUNK
    ns = CHUNK // SUB_CHUNK
    qc = q.reshape(b_, h_, n, CHUNK, dk)
    kc = k.reshape(b_, h_, n, CHUNK, dk)
    vc = v.reshape(b_, h_, n, CHUNK, dv)
    b = jnp.cumsum(log_f.reshape(b_, h_, n, CHUNK, dk), axis=3)
    b_last = b[:, :, :, -1]

    k_to_end = kc * jnp.exp(b_last[:, :, :, None] - b)
    u_chunk = jnp.einsum('bhnck,bhncv->bhnkv', k_to_end, vc)
    decay = jnp.exp(b_last)

    def step(state, inp):
        d, u = inp
        return d[..., None] * state + u, state

    s0 = jnp.zeros((b_, h_, dk, dv), q.dtype)
    _, s_prev = lax.scan(step, s0, (jnp.moveaxis(decay, 2, 0), jnp.moveaxis(u_chunk, 2, 0)))
    s_prev = jnp.moveaxis(s_prev, 0, 2)
    o_state = jnp.einsum('bhnck,bhnkv->bhncv', qc * jnp.exp(b), s_prev)

    bs = b.reshape(b_, h_, n, ns, SUB_CHUNK, dk)
    qs = qc.reshape(b_, h_, n, ns, SUB_CHUNK, dk)
    ks = kc.reshape(b_, h_, n, ns, SUB_CHUNK, dk)
    vs = vc.reshape(b_, h_, n, ns, SUB_CHUNK, dv)
    b_ref = jnp.concatenate([jnp.zeros_like(bs[:, :, :, :1, -1]), bs[:, :, :, :-1, -1]], axis=3)
    q_ref = qs * jnp.exp(bs - b_ref[..., None, :])
    sub_id = jnp.arange(CHUNK) // SUB_CHUNK
    earlier = sub_id[None, :] < jnp.arange(ns)[:, None]
    k_exp = jnp.where(earlier[:, :, None],
                      b_ref[:, :, :, :, None, :] - b[:, :, :, None, :, :], -jnp.inf)
    k_ref = kc[:, :, :, None] * jnp.exp(k_exp)
    a_cross = jnp.einsum('bhnitk,bhnisk->bhnits', q_ref, k_ref)
    o_cross = jnp.einsum('bhnits,bhnsv->bhnitv', a_cross, vc)

    tril = jnp.tril(jnp.ones((SUB_CHUNK, SUB_CHUNK), bool))
    rel = jnp.where(tril[:, :, None], bs[..., :, None, :] - bs[..., None, :, :], -jnp.inf)
    a_diag = jnp.sum(qs[..., :, None, :] * ks[..., None, :, :] * jnp.exp(rel), axis=-1)
    o_diag = jnp.einsum('bhnits,bhnisv->bhnitv', a_diag, vs)

    o = o_state + (o_cross + o_diag).reshape(b_, h_, n, CHUNK, dv)
    return o.reshape(b_, h_, s_, dv)


def hybrid_mixer(h, w_in, pool_w, pool_scale, lb, hgrn_norm, w_pool_proj, w_hgrn_proj, w_out):
    b_, s_, _ = h.shape
    z = h @ w_in
    splits = [POOL_WIDTH, POOL_WIDTH + HGRN_FORGET, POOL_WIDTH + 2 * HGRN_FORGET,
              POOL_WIDTH + 2 * HGRN_FORGET + HGRN_INPUT, POOL_WIDTH + 2 * HGRN_FORGET + 2 * HGRN_INPUT]
    u_pool, q_pre, f_pre, i_in, og, gates = jnp.split(z, splits, axis=-1)

    pool_out = causal_pool_mixer(u_pool, pool_w, pool_scale)

    def heads(t, d):
        return t.reshape(b_, s_, HGRN_HEADS, d).transpose(0, 2, 1, 3).astype(jnp.float32)

    fz = heads(f_pre, HGRN_EXPAND)
    lbh = lb.astype(jnp.float32).reshape(HGRN_HEADS, 1, HGRN_EXPAND)
    log_f = jnp.logaddexp(jnp.log(lbh), jnp.log1p(-lbh) + jax.nn.log_sigmoid(fz))
    k = (1.0 - lbh) * jax.nn.sigmoid(-fz)
    q = jax.nn.silu(heads(q_pre, HGRN_EXPAND))
    v = heads(i_in, HGRN_HEAD_DIM)
    o = hgrn2_chunkwise(q, k, v, log_f)
    o = o * lax.rsqrt(jnp.mean(o * o, axis=-1, keepdims=True) + EPS)
    o = o.transpose(0, 2, 1, 3).reshape(b_, s_, HGRN_INPUT) * hgrn_norm.astype(jnp.float32)
    hgrn_out = (o * jax.nn.silu(og.astype(jnp.float32))).astype(h.dtype)

    g = jax.nn.sigmoid(gates).reshape(b_, s_, N_BRANCHES, D_MODEL)
    merged = g[:, :, 0] * (pool_out @ w_pool_proj) + g[:, :, 1] * (hgrn_out @ w_hgrn_proj)
    return merged @ w_out


def setup_inputs(seed: int = 0) -> dict:
    key = jax.random.key(seed)
    ks = jax.random.split(key, 20)

    def nrm(k, shape, fan_in):
        return jax.random.normal(k, shape, jnp.float32) * fan_in ** -0.5

    def gain(k, shape):
        return 1.0 + 0.05 * jax.random.normal(k, shape, jnp.float32)

    L, D = DEPTH, D_MODEL
    return {
        'x': jax.random.normal(ks[0], (BATCH, SEQ, D), jnp.float32),
        'ffn1_norm': gain(ks[1], (L, D)),
        'ffn1_w_gate': nrm(ks[2], (L, D, D_FF), D),
        'ffn1_w_up': nrm(ks[3], (L, D, D_FF), D),
        'ffn1_w_down': nrm(ks[4], (L, D_FF, D), D_FF),
        'mix_norm': gain(ks[5], (L, D)),
        'w_in': nrm(ks[6], (L, D, D_IN), D),
        'pool_w': nrm(ks[7], (L, POOL_GROUPS, POOL_GROUP_DIM, POOL_GROUP_DIM), POOL_GROUP_DIM),
        'pool_scale': gain(ks[8], (L, POOL_WIDTH)),
        'lb_logits': 0.5 * jax.random.normal(ks[9], (L, HGRN_FORGET), jnp.float32),
        'hgrn_norm': gain(ks[10], (L, HGRN_INPUT)),
        'w_pool_proj': nrm(ks[11], (L, POOL_WIDTH, D), POOL_WIDTH),
        'w_hgrn_proj': nrm(ks[12], (L, HGRN_INPUT, D), HGRN_INPUT),
        'w_out': nrm(ks[13], (L, D, D), D),
        'ffn2_norm': gain(ks[14], (L, D)),
        'ffn2_w_gate': nrm(ks[15], (L, D, D_FF), D),
        'ffn2_w_up': nrm(ks[16], (L, D, D_FF), D),
        'ffn2_w_down': nrm(ks[17], (L, D_FF, D), D_FF),
        'final_norm': gain(ks[18], (D,)),
    }


def reference(x, ffn1_norm, ffn1_w_gate, ffn1_w_up, ffn1_w_down, mix_norm, w_in, pool_w,
              pool_scale, lb_logits, hgrn_norm, w_pool_proj, w_hgrn_proj, w_out,
              ffn2_norm, ffn2_w_gate, ffn2_w_up, ffn2_w_down, final_norm):
    lb_all = jnp.cumsum(jax.nn.softmax(lb_logits.astype(jnp.float32), axis=0), axis=0)
    lb_all = lb_all - lb_all[:1]
    for l in range(DEPTH):
        x = x + 0.5 * swiglu(rmsnorm(x, ffn1_norm[l]), ffn1_w_gate[l], ffn1_w_up[l], ffn1_w_down[l])
        h = rmsnorm(x, mix_norm[l])
        x = x + hybrid_mixer(h, w_in[l], pool_w[l], pool_scale[l], lb_all[l], hgrn_norm[l],
                             w_pool_proj[l], w_hgrn_proj[l], w_out[l])
        x = x + 0.5 * swiglu(rmsnorm(x, ffn2_norm[l]), ffn2_w_gate[l], ffn2_w_up[l], ffn2_w_down[l])
    return rmsnorm(x, final_norm)
```

```python
import contextlib
import numpy as np
import concourse.bass as bass
import concourse.mybir as mybir
from concourse.bass_utils import run_bass_kernel_spmd

F32 = mybir.dt.float32
BF16 = mybir.dt.bfloat16
AF = mybir.ActivationFunctionType
ALU = mybir.AluOpType
ENGS = ["pe", "act", "dve", "pool", "sp"]
ENGOBJ = {"pe": "tensor", "act": "scalar", "dve": "vector", "pool": "gpsimd", "sp": "sync"}

D = 1024
DFF = 2816
DIN = 4608
T = 2048
TT = 512
NT = T // TT
NCORES = 8
EPS = 1e-6
FFN_GROUPS = [(0, 4), (4, 4), (8, 4), (12, 4), (16, 4), (20, 2)]
STAGES = {"lb", "ffn", "normmix", "m1"}
BSTAGES = {"m2", "ffn"}
STATE_WORDS = 36864


class Op:
    __slots__ = ("eng", "idx", "fn", "deps", "dma", "signaled", "val", "semkey")


class Buf:
    __slots__ = ("name", "writers", "readers")

    def __init__(self, name=""):
        self.name = name
        self.writers = {}
        self.readers = {}


class _Rec:
    def __init__(self):
        self.call = None

    def __getattr__(self, name):
        def f(*args, **kwargs):
            self.call = (name, args, kwargs)
            return None
        return f


class Prog:
    def __init__(self, nc):
        self.nc = nc
        self.ops = {e: [] for e in ENGS}
        self.dma_keys = []
        self.last_dma = {}

    def barrier(self, skip=()):
        lasts = [v for k, v in self.last_dma.items() if k not in skip]
        for e in ENGS:
            for o in reversed(self.ops[e]):
                if o.fn is not None and not (o.dma is not None and ("dma_" + o.dma) in skip):
                    lasts.append(o)
                    break
        for e in ENGS:
            self.add(e, None, deps=lasts, force_deps=True)

    def add(self, eng, fn, reads=(), writes=(), deps=(), dma=None, force_deps=False):
        op = Op()
        op.eng = eng
        op.idx = len(self.ops[eng])
        if fn is not None:
            rec = _Rec()
            fn(rec)
            assert rec.call is not None
            fn = rec.call
        op.fn = fn
        op.dma = dma
        op.signaled = False
        op.val = None
        dl = [d for d in deps if d is not None]
        if dma is not None and ("dma_" + dma) in self.last_dma:
            dl.append(self.last_dma["dma_" + dma])
        for b in reads:
            dl.extend(b.writers.values())
        for b in writes:
            dl.extend(b.writers.values())
            dl.extend(b.readers.values())
        fl = []
        for d in dl:
            if d is op:
                continue
            if d.dma is None and d.eng == eng and not force_deps:
                if eng == "pe":
                    continue
                if eng == "sp":
                    continue
                if op.idx - d.idx > 2:
                    continue
            fl.append(d)
        op.deps = fl
        if dma is not None:
            op.semkey = "dma_" + dma
            if op.semkey not in self.dma_keys:
                self.dma_keys.append(op.semkey)
            self.last_dma[op.semkey] = op
        else:
            op.semkey = "eng_" + eng
        self.ops[eng].append(op)
        for b in reads:
            b.readers[eng if dma is None else "dma_" + dma] = op
        for b in writes:
            b.writers[eng if dma is None else "dma_" + dma] = op
            b.readers = {}
        return op

    def emit(self):
        nc = self.nc
        for e in ENGS:
            for op in self.ops[e]:
                for d in op.deps:
                    d.signaled = True
                if op.dma is not None:
                    op.signaled = True
        counters = {}
        for e in ENGS:
            for op in self.ops[e]:
                if op.signaled:
                    inc = 16 if (op.dma is not None and not op.dma.startswith("cc")) else 1
                    counters[op.semkey] = counters.get(op.semkey, 0) + inc
                    op.val = counters[op.semkey]
        keys = ["eng_" + e for e in ENGS if any(o.signaled and o.dma is None for o in self.ops[e])]
        keys += self.dma_keys
        with contextlib.ExitStack() as st:
            sems = {k: st.enter_context(nc.semaphore(k)) for k in keys}
            block = st.enter_context(nc.Block())

            def make_body(e):
                def body(eng):
                    waited = {}
                    for op in self.ops[e]:
                        need = {}
                        for d in op.deps:
                            need[d.semkey] = max(need.get(d.semkey, 0), d.val)
                        for k, v in need.items():
                            if waited.get(k, 0) < v:
                                eng.wait_ge(sems[k], v)
                                waited[k] = v
                        if op.fn is None:
                            continue
                        name, args, kwargs = op.fn
                        ins = getattr(eng, name)(*args, **kwargs)
                        if op.signaled:
                            ins.then_inc(sems[op.semkey], 16 if (op.dma is not None and not op.dma.startswith("cc")) else 1)
                return body

            for e in ENGS:
                if self.ops[e]:
                    getattr(block, ENGOBJ[e])(make_body(e))


class Ctx:
    pass


def _bf(ap_f32):
    return ap_f32.bitcast(BF16)


def setup_common(nc, P, st):
    c = Ctx()
    c.nc = nc
    c.P = P
    arena = st.enter_context(nc.sbuf_tensor("arena", [128, 49152], F32))
    c.arena = arena
    c.banks = [st.enter_context(nc.psum_tensor("bank%d" % i, [128, 512], F32)) for i in range(8)]
    c.bankb = [Buf("bank%d" % i) for i in range(8)]
    c.X = arena[:, 0:16384].rearrange("p (c t) -> p c t", c=8)
    c.H = _bf(arena[:, 16384:24576]).rearrange("p (c t) -> p c t", c=8)
    c.OL = arena[:, 24576:32768].rearrange("p (h t) -> p h t", h=4)
    c.QB = _bf(arena[:, 32768:36864]).rearrange("p (h t) -> p h t", h=4)
    c.Xb = [[Buf("X%d_%d" % (n, t)) for t in range(NT)] for n in range(8)]
    c.Hb = [Buf("H%d" % t) for t in range(NT)]
    c.OLb = [[Buf("OL") for t in range(NT)] for h in range(4)]
    c.QBb = [[Buf("QB") for t in range(NT)] for h in range(4)]
    cb = 48128
    c.CON = arena[:, cb:cb + 64]
    c.COEF = arena[:, cb + 64:cb + 72]
    c.FIX = arena[:, cb + 72:cb + 136]
    c.ONES = _bf(arena[:, cb + 136:cb + 200])
    c.IDENT = _bf(arena[:, cb + 200:cb + 264])
    c.MASK = arena[:, cb + 264:cb + 392]
    c.EPSC = arena[:, cb + 392:cb + 393]
    c.ONEC = arena[:, cb + 393:cb + 394]
    c.LB = arena[:, cb + 400:cb + 404]
    c.OML = arena[:, cb + 404:cb + 408]
    c.NOML = arena[:, cb + 408:cb + 412]
    c.ZEROC = arena[:, cb + 412:cb + 413]
    c.UOUT = arena[:, cb + 416:cb + 928].rearrange("p (h v) -> p h v", h=4)
    c.DOUT = arena[:, cb + 928:cb + 932]
    c.conb = Buf("con")
    c.coefb = Buf("coef")
    c.constb = Buf("const")
    c.lbb = Buf("lb")
    c.uoutb = Buf("uout")
    c.S0 = 36864
    return c


def emit_consts(c):
    P = c.P
    P.add("dve", lambda e: e.memset(c.ONES, 1.0), writes=[c.constb])
    P.add("dve", lambda e: e.memset(c.IDENT, 0.0), writes=[c.constb])
    P.add("dve", lambda e: e.memset(c.MASK, 1.0), writes=[c.constb])
    P.add("dve", lambda e: e.memset(c.EPSC, EPS), writes=[c.constb])
    P.add("dve", lambda e: e.memset(c.ONEC, 1.0), writes=[c.constb])
    P.add("dve", lambda e: e.memset(c.ZEROC, 0.0), writes=[c.constb])
    P.add("pool", lambda e: e.affine_select(out=c.IDENT, in_=c.IDENT, pattern=[[-1, 128]], compare_op=ALU.not_equal,
                                            fill=1.0, base=0, channel_multiplier=1), reads=[c.constb], writes=[c.constb])
    P.add("pool", lambda e: e.affine_select(out=c.MASK, in_=c.MASK, pattern=[[1, 128]], compare_op=ALU.is_ge,
                                            fill=0.0, base=0, channel_multiplier=-1), reads=[c.constb], writes=[c.constb])
    P.add("pool", lambda e: e.memset(c.MASK[0:64, 64:128], 0.0), reads=[c.constb], writes=[c.constb])


def emit_lb(c, col0=32, col1=36, on_col=40, layer=None):
    P = c.P
    if layer == 0:
        P.add("dve", lambda e: e.memset(c.LB, 0.0), writes=[c.lbb])
    else:
        P.add("dve", lambda e: e.tensor_tensor(out=c.LB, in0=c.CON[:, col1:col1 + 4], in1=c.CON[:, col0:col0 + 4], op=ALU.subtract),
              reads=[c.conb], writes=[c.lbb])
        P.add("act", lambda e: e.activation(out=c.LB, in_=c.LB, func=AF.Sigmoid), reads=[c.lbb], writes=[c.lbb])
        if layer is None:
            P.add("dve", lambda e: e.tensor_scalar(out=c.LB, in0=c.LB, scalar1=c.CON[:, on_col:on_col + 1], scalar2=None, op0=ALU.mult),
                  reads=[c.lbb, c.conb], writes=[c.lbb])
    P.add("dve", lambda e: e.tensor_scalar(out=c.OML, in0=c.LB, scalar1=-1.0, scalar2=1.0, op0=ALU.mult, op1=ALU.add),
          reads=[c.lbb], writes=[c.lbb])
    P.add("dve", lambda e: e.tensor_scalar(out=c.NOML, in0=c.LB, scalar1=-1.0, scalar2=None, op0=ALU.add),
          reads=[c.lbb], writes=[c.lbb])


def tl(t):
    return slice(t * TT, (t + 1) * TT)


def norm_parts(c, gcol0, sq, sqb, rs, rsb, t, dst_fn=None):
    P = c.P
    nbank = 6

    def p0():
        P.add("act", lambda e: e.activation(out=sq, in_=c.X[:, :, tl(t)], func=AF.Square),
              reads=[c.Xb[n][t] for n in range(8)], writes=[sqb])

    def p1():
        for k in range(8):
            P.add("pe", lambda e, k=k: e.matmul(c.banks[nbank][:], lhsT=c.ONES, rhs=sq[:, k, :], start=(k == 0), stop=(k == 7)),
                  reads=[sqb, c.constb], writes=[c.bankb[nbank]])
        P.add("act", lambda e: e.activation(out=rs, in_=c.banks[nbank][:], func=AF.Ln, scale=1.0 / D, bias=c.EPSC),
              reads=[c.bankb[nbank], c.constb], writes=[rsb])
        P.add("act", lambda e: e.activation(out=rs, in_=rs, func=AF.Exp, scale=-0.5), reads=[rsb], writes=[rsb])

    def stt(ks):
        for k in ks:
            P.add("dve", lambda e, k=k: e.scalar_tensor_tensor(out=c.H[:, k, tl(t)], in0=c.X[:, k, tl(t)],
                                                              scalar=c.CON[:, gcol0 + k:gcol0 + k + 1], in1=rs,
                                                              op0=ALU.mult, op1=ALU.mult),
                  reads=[c.Xb[k][t], rsb, c.conb], writes=[c.Hb[t]])

    def p2():
        if dst_fn is None:
            stt(range(0, 4))
        else:
            dst_fn(t, rs, rsb)

    def p3():
        if dst_fn is None:
            stt(range(4, 8))
    return [p0, p1, p2, p3]


def emit_norm(c, gcol0, sq, sqb, rs, rsb, dst_fn=None, tiles=None):
    for t in (range(NT) if tiles is None else tiles):
        for p in norm_parts(c, gcol0, sq, sqb, rs, rsb, t, dst_fn=dst_fn):
            p()


def wload(c, out_ap, in_ap, buf, key="w"):
    return c.P.add("pool", lambda e: e.dma_start(out=out_ap, in_=in_ap), writes=[buf], dma=key)


def emit_ffn(c, wg, wu, wd, gcol0):
    P = c.P
    a = c.arena
    s = 24576
    WG = [_bf(a[:, s + i * 2048: s + (i + 1) * 2048]).rearrange("p (k n) -> p k n", k=8) for i in range(2)]
    s += 4096
    WU = [_bf(a[:, s + i * 2048: s + (i + 1) * 2048]).rearrange("p (k n) -> p k n", k=8) for i in range(2)]
    s += 4096
    WD = [_bf(a[:, s + i * 2048: s + (i + 1) * 2048]).rearrange("p (j n) -> p j n", j=4) for i in range(2)]
    s += 4096
    AB = [_bf(a[:, s + i * 1024: s + (i + 1) * 1024]).rearrange("p (j n) -> p j n", j=4) for i in range(2)]
    s += 2048
    SG = [_bf(a[:, s + i * 256: s + (i + 1) * 256]) for i in range(2)]
    s += 512
    SQ = _bf(a[:, s:s + 2048]).rearrange("p (k n) -> p k n", k=8)
    s += 2048
    RS = a[:, s:s + 512]
    s += 512
    WGb = [Buf("WG"), Buf("WG")]
    WUb = [Buf("WU"), Buf("WU")]
    WDb = [Buf("WD"), Buf("WD")]
    ABb = [Buf("AB"), Buf("AB")]
    SGb = [Buf("SG"), Buf("SG")]
    SQb = Buf("SQ")
    RSb = Buf("RS")

    wgv = wg.rearrange("(k p) n -> p k n", p=128)
    wuv = wu.rearrange("(k p) n -> p k n", p=128)
    wdv = wd.rearrange("(j p) n -> p j n", p=128)

    def load_group(gi):
        j0, G = FFN_GROUPS[gi]
        b = gi % 2
        wload(c, WG[b][:, :, 0:G * 128], wgv[:, :, j0 * 128:(j0 + G) * 128], WGb[b], "g%d" % b)
        wload(c, WU[b][:, :, 0:G * 128], wuv[:, :, j0 * 128:(j0 + G) * 128], WUb[b], "u%d" % b)
        wload(c, WD[b][:, 0:G, :], wdv[:, j0:j0 + G, :], WDb[b], "d%d" % b)

    P.barrier()
    load_group(0)

    pending = None
    cnt = 0
    units = [(gi, t) for gi in range(len(FFN_GROUPS)) for t in range(NT)]
    for ui, (gi, t) in enumerate(units):
        j0, G = FFN_GROUPS[gi]
        b = gi % 2
        ab = ui % 2
        nparts = [None] * 4
        if gi == 0:
            if t == 0:
                emit_norm(c, gcol0, SQ, SQb, RS, RSb, tiles=[0])
            if t + 1 < NT:
                nparts = norm_parts(c, gcol0, SQ, SQb, RS, RSb, t + 1)
        for jl in range(G):
            gb = cnt % 2
            ub = 2 + cnt % 2
            sgi = cnt % 2
            cnt += 1
            for k in range(8):
                P.add("pe", lambda e, k=k, jl=jl, b=b, gb=gb, t=t: e.matmul(c.banks[gb][:], lhsT=WG[b][:, k, jl * 128:(jl + 1) * 128],
                                                                            rhs=c.H[:, k, tl(t)], start=(k == 0), stop=(k == 7)),
                      reads=[WGb[b], c.Hb[t]], writes=[c.bankb[gb]])
            for k in range(8):
                P.add("pe", lambda e, k=k, jl=jl, b=b, ub=ub, t=t: e.matmul(c.banks[ub][:], lhsT=WU[b][:, k, jl * 128:(jl + 1) * 128],
                                                                            rhs=c.H[:, k, tl(t)], start=(k == 0), stop=(k == 7)),
                      reads=[WUb[b], c.Hb[t]], writes=[c.bankb[ub]])
            P.add("act", lambda e, gb=gb, sgi=sgi: e.activation(out=SG[sgi], in_=c.banks[gb][:], func=AF.Silu),
                  reads=[c.bankb[gb]], writes=[SGb[sgi]])
            P.add("dve", lambda e, ub=ub, sgi=sgi, ab=ab, jl=jl: e.tensor_tensor(out=AB[ab][:, jl, :], in0=c.banks[ub][:], in1=SG[sgi], op=ALU.mult),
                  reads=[c.bankb[ub], SGb[sgi]], writes=[ABb[ab]])
            if jl < 4 and nparts[jl] is not None:
                nparts[jl]()
        if pending is not None:
            pending()
        if t == 0 and gi + 1 < len(FFN_GROUPS):
            load_group(gi + 1)

        def down(gi=gi, t=t, G=G, b=b, ab=ab):
            for n in range(8):
                db = 4 + n % 2
                for jl in range(G):
                    P.add("pe", lambda e, n=n, jl=jl, db=db: e.matmul(c.banks[db][:], lhsT=WD[b][:, jl, n * 128:(n + 1) * 128],
                                                                      rhs=AB[ab][:, jl, :], start=(jl == 0), stop=(jl == G - 1)),
                          reads=[WDb[b], ABb[ab]], writes=[c.bankb[db]])
                P.add("dve", lambda e, n=n, db=db: e.scalar_tensor_tensor(out=c.X[:, n, tl(t)], in0=c.banks[db][:], scalar=0.5,
                                                                          in1=c.X[:, n, tl(t)], op0=ALU.mult, op1=ALU.add),
                      reads=[c.bankb[db], c.Xb[n][t]], writes=[c.Xb[n][t]])
        pending = down
    pending()


def emit_m1(c, w_in, final_fn=None, norm_gcol=None):
    P = c.P
    a = c.arena
    s = c.S0
    winv = w_in.rearrange("(k p) n -> p k n", p=128)

    def take(n):
        nonlocal s
        r = a[:, s:s + n]
        s += n
        return r
    WH = [_bf(take(1536)).rearrange("p (k j n) -> p k j n", k=8, j=3) for i in range(2)]
    WHb = [Buf("WH"), Buf("WH")]
    QTd = [_bf(take(256)) for i in range(2)]
    KTd = [_bf(take(256)) for i in range(2)]
    QBl = [take(512) for i in range(2)]
    Vt = [_bf(take(256)).rearrange("p (b v) -> p b v", b=4) for i in range(2)]
    KBd = [_bf(take(256)).rearrange("p (b v) -> p b v", b=4) for i in range(2)]
    ub_ = [Buf("unit") for i in range(2)]
    vtb = [Buf("vt") for i in range(2)]
    kbb = [Buf("kb") for i in range(2)]
    Q = take(512)
    K = take(512)
    KBT = _bf(take(256))
    Qb_, Kb_, KBTb = Buf("Q"), Buf("K"), Buf("KBT")
    SC = [take(512) for i in range(4)]
    SCb = [Buf("sc") for i in range(4)]
    BT = [take(516) for i in range(2)]
    BTb = [Buf("bt") for i in range(2)]
    ATs = [_bf(take(64)) for i in range(4)]
    ATb = [Buf("ats") for i in range(4)]
    S = [take(128) for i in range(3)]
    if final_fn is not None:
        S += [a[:, 48128 + 600 + i * 128: 48128 + 600 + (i + 1) * 128] for i in range(2)]
    NS = len(S)
    Sb = [Buf("S") for i in range(NS)]
    DEC = [take(8) for i in range(2)]
    DECb = [Buf("dec") for i in range(2)]
    assert s <= 48128, s
    sci = [0]

    def scratch():
        i = sci[0] % 4
        sci[0] += 1
        return SC[i], SCb[i]

    def load_head(h):
        b = h % 2
        for j, off in enumerate((512, 1024, 1536)):
            wload(c, WH[b][:, :, j, :], winv[:, :, off + h * 128: off + (h + 1) * 128], WHb[b], "gud"[j] + str(b))

    P.barrier()
    load_head(0)
    BK_Q, BK_F, BK_V, BK_AT, BK_O, BK_U, BK_T = 0, 1, 2, 3, 4, 5, 7
    pst = c.banks[BK_T][:].bitcast(BF16)
    atc = [0]
    sidx = [0]

    def v3(x):
        return x.rearrange("p (c j) -> p c j", j=64)

    if norm_gcol is not None:
        NSQ = _bf(a[:, 24576 + 3 * 2048: 24576 + 4 * 2048]).rearrange("p (k n) -> p k n", k=8)
        NRS = a[:, 24576 + 2 * 2048 + 1536: 24576 + 3 * 2048]
        NSQb, NRSb = Buf("nsq"), Buf("nrs")
    else:
        NSQb = NRSb = None

    def front_segs(ui):
        h, t = divmod(ui, NT)
        u = ui % 2
        wb = h % 2
        bt = BT[t % 2]
        btb = BTb[t % 2]
        btm = bt[:, 1:513].rearrange("p (c j) -> p c j", j=64)
        bts = bt[:, 0:512].rearrange("p (c j) -> p c j", j=64)
        rmid = btm[:, :, 31:32].broadcast_to([128, 8, 64])
        bst = bts[:, :, 0:1].broadcast_to([128, 8, 64])
        ben = btm[:, :, 63:64].broadcast_to([128, 8, 64])
        st = {}

        def seg0():
            if norm_gcol is not None and h == 0:
                emit_norm(c, norm_gcol, NSQ, NSQb, NRS, NRSb, tiles=[t])
            if t == 0:
                P.add("dve", lambda e: e.memset(BT[0][:, 0:1], 0.0), writes=[BTb[0]])
            for k in range(8):
                P.add("pe", lambda e, k=k: e.matmul(c.banks[BK_Q][:], lhsT=WH[wb][:, k, 0, :], rhs=c.H[:, k, tl(t)], start=(k == 0), stop=(k == 7)),
                      reads=[WHb[wb], c.Hb[t]], writes=[c.bankb[BK_Q]])
            for k in range(8):
                P.add("pe", lambda e, k=k: e.matmul(c.banks[BK_F][:], lhsT=WH[wb][:, k, 1, :], rhs=c.H[:, k, tl(t)], start=(k == 0), stop=(k == 7)),
                      reads=[WHb[wb], c.Hb[t]], writes=[c.bankb[BK_F]])
            for blk in range(4):
                for k in range(8):
                    P.add("pe", lambda e, k=k, blk=blk: e.matmul(c.banks[BK_V][:, blk * 128:(blk + 1) * 128],
                                                                 lhsT=c.H[:, k, t * TT + blk * 128: t * TT + (blk + 1) * 128],
                                                                 rhs=WH[wb][:, k, 2, :], start=(k == 0), stop=(k == 7)),
                          reads=[WHb[wb], c.Hb[t]], writes=[c.bankb[BK_V]])
            if t == 0 and h + 1 < 4:
                load_head(h + 1)
            sq_, sqb_ = scratch()
            P.add("act", lambda e: e.activation(out=sq_, in_=c.banks[BK_Q][:], func=AF.Sigmoid), reads=[c.bankb[BK_Q]], writes=[sqb_])
            sig, sigb = scratch()
            st["sig"], st["sigb"] = sig, sigb
            P.add("act", lambda e: e.activation(out=sig, in_=c.banks[BK_F][:], func=AF.Sigmoid), reads=[c.bankb[BK_F]], writes=[sigb])
            P.add("act", lambda e: e.activation(out=Vt[u], in_=c.banks[BK_V][:].rearrange("p (b v) -> p b v", b=4), func=AF.Copy),
                  reads=[c.bankb[BK_V]], writes=[vtb[u]])
            P.add("dve", lambda e: e.tensor_tensor(out=Q, in0=c.banks[BK_Q][:], in1=sq_, op=ALU.mult), reads=[c.bankb[BK_Q], sqb_], writes=[Qb_])
            P.add("dve", lambda e: e.tensor_scalar(out=K, in0=sig, scalar1=c.NOML[:, h:h + 1], scalar2=c.OML[:, h:h + 1], op0=ALU.mult, op1=ALU.add),
                  reads=[sigb, c.lbb], writes=[Kb_])

        def seg1():
            sig, sigb = st["sig"], st["sigb"]
            lf, lfb = scratch()
            P.add("act", lambda e: e.activation(out=lf, in_=sig, func=AF.Ln, scale=c.OML[:, h:h + 1], bias=c.LB[:, h:h + 1]),
                  reads=[sigb, c.lbb], writes=[lfb])
            P.add("dve", lambda e: e.tensor_tensor_scan(bt[:, 1:513], c.ONEC.broadcast_to([128, 512]), lf, bt[:, 0:1], ALU.mult, ALU.add),
                  reads=[lfb, btb, c.constb], writes=[btb])
            if t + 1 < NT:
                nb = BT[(t + 1) % 2]
                P.add("dve", lambda e: e.tensor_copy(out=nb[:, 0:1], in_=bt[:, 512:513]), reads=[btb], writes=[BTb[(t + 1) % 2]])
            d1, d1b = scratch()
            P.add("dve", lambda e: e.tensor_tensor(out=v3(d1), in0=btm, in1=rmid, op=ALU.subtract), reads=[btb], writes=[d1b])
            e1, e1b = scratch()
            P.add("act", lambda e: e.activation(out=e1, in_=d1, func=AF.Exp), reads=[d1b], writes=[e1b])
            P.add("dve", lambda e: e.tensor_tensor(out=QTd[u], in0=Q, in1=e1, op=ALU.mult), reads=[Qb_, e1b], writes=[ub_[u]])
            e2, e2b = scratch()
            P.add("act", lambda e: e.activation(out=e2, in_=d1, func=AF.Exp, scale=-1.0), reads=[d1b], writes=[e2b])
            P.add("pool", lambda e: e.tensor_tensor(out=KTd[u], in0=K, in1=e2, op=ALU.mult), reads=[Kb_, e2b], writes=[ub_[u]])

        def seg2():
            d2, d2b = scratch()
            P.add("dve", lambda e: e.tensor_tensor(out=v3(d2), in0=btm, in1=bst, op=ALU.subtract), reads=[btb], writes=[d2b])
            e3, e3b = scratch()
            P.add("act", lambda e: e.activation(out=e3, in_=d2, func=AF.Exp), reads=[d2b], writes=[e3b])
            P.add("pool", lambda e: e.tensor_tensor(out=QBl[u], in0=Q, in1=e3, op=ALU.mult), reads=[Qb_, e3b], writes=[ub_[u]])
            d3, d3b = scratch()
            P.add("dve", lambda e: e.tensor_tensor(out=v3(d3), in0=btm, in1=ben, op=ALU.subtract), reads=[btb], writes=[d3b])
            e4, e4b = scratch()
            P.add("act", lambda e: e.activation(out=e4, in_=d3, func=AF.Exp, scale=-1.0), reads=[d3b], writes=[e4b])
            P.add("pool", lambda e: e.tensor_tensor(out=KBT, in0=K, in1=e4, op=ALU.mult), reads=[Kb_, e4b], writes=[KBTb])

        def seg3():
            e5, e5b = scratch()
            P.add("act", lambda e: e.activation(out=e5, in_=bt[:, 1:513], func=AF.Exp), reads=[btb], writes=[e5b])
            P.add("pool", lambda e: e.tensor_tensor(out=c.QB[:, h, tl(t)], in0=Q, in1=e5, op=ALU.mult), reads=[Qb_, e5b], writes=[c.QBb[h][t]])
            dec = DEC[u]
            P.add("dve", lambda e: e.tensor_tensor(out=dec, in0=bt[:, 64:513:64], in1=bt[:, 0:512:64], op=ALU.subtract), reads=[btb], writes=[DECb[u]])
            P.add("act", lambda e: e.activation(out=dec, in_=dec, func=AF.Exp), reads=[DECb[u]], writes=[DECb[u]])
            if t == NT - 1:
                P.add("act", lambda e: e.activation(out=c.DOUT[:, h:h + 1], in_=bt[:, 512:513], func=AF.Exp), reads=[btb], writes=[c.uoutb])
            for blk in range(4):
                P.add("pe", lambda e, blk=blk: e.transpose(pst[:, blk * 128:(blk + 1) * 128], KBT[:, blk * 128:(blk + 1) * 128], c.IDENT),
                      reads=[KBTb, c.constb], writes=[c.bankb[BK_T]])
            P.add("act", lambda e: e.activation(out=KBd[u], in_=pst[:, 0:512].rearrange("p (b v) -> p b v", b=4), func=AF.Copy),
                  reads=[c.bankb[BK_T]], writes=[kbb[u]])
        return [seg0, seg1, seg2, seg3]

    def back_blocks(ui):
        h, t = divmod(ui, NT)
        u = ui % 2
        info = {}

        def mk1(blk):
            def f():
                if t == 0 and blk == 0:
                    sidx[0] = 0
                    P.add("dve", lambda e: e.memset(S[0], 0.0), writes=[Sb[0]])
                ai = atc[0] % 4
                atc[0] += 1
                tok = slice(blk * 128, (blk + 1) * 128)
                atp = c.banks[BK_AT][:, ai * 128:(ai + 1) * 128]
                P.add("pe", lambda e: e.matmul(atp, lhsT=KTd[u][:, tok], rhs=QTd[u][:, tok], start=True, stop=True),
                      reads=[ub_[u]], writes=[c.bankb[BK_AT]])
                P.add("dve", lambda e: e.tensor_tensor(out=ATs[ai], in0=atp, in1=c.MASK, op=ALU.mult),
                      reads=[c.bankb[BK_AT], c.constb], writes=[ATb[ai]])
                s0, s1, s2 = sidx[0] % NS, (sidx[0] + 1) % NS, (sidx[0] + 2) % NS
                sidx[0] += 2
                ubk = [BK_U, 6]
                up = [c.banks[ubk[i]][:, (ai % 4) * 128:(ai % 4 + 1) * 128] for i in range(2)]
                for i in range(2):
                    pr = slice(i * 64, (i + 1) * 64)
                    P.add("pe", lambda e, pr=pr, i=i: e.matmul(up[i], lhsT=KBd[u][pr, blk, :], rhs=Vt[u][pr, blk, :], start=True, stop=True),
                          reads=[kbb[u], vtb[u]], writes=[c.bankb[ubk[i]]])
                cA = blk * 2
                P.add("dve", lambda e: e.scalar_tensor_tensor(out=S[s1], in0=S[s0], scalar=DEC[u][:, cA:cA + 1], in1=up[0], op0=ALU.mult, op1=ALU.add),
                      reads=[Sb[s0], DECb[u], c.bankb[BK_U]], writes=[Sb[s1]])
                P.add("dve", lambda e: e.scalar_tensor_tensor(out=S[s2], in0=S[s1], scalar=DEC[u][:, cA + 1:cA + 2], in1=up[1], op0=ALU.mult, op1=ALU.add),
                      reads=[Sb[s1], DECb[u], c.bankb[6]], writes=[Sb[s2]])
                info[blk] = (ai, s0, s1)
                if t == NT - 1 and blk == 3:
                    sl = sidx[0] % NS
                    if final_fn is None:
                        P.add("dve", lambda e: e.tensor_copy(out=c.UOUT[:, h, :], in_=S[sl]), reads=[Sb[sl]], writes=[c.uoutb])
                    else:
                        final_fn(h, S[sl], Sb[sl])
            return f

        def mk2(blk):
            def f():
                ai, s0, s1 = info[blk]
                op_ = c.banks[BK_O][:, ai * 128:(ai + 1) * 128]
                P.add("pe", lambda e: e.matmul(op_, lhsT=Vt[u][:, blk, :], rhs=ATs[ai], start=True, stop=False),
                      reads=[vtb[u], ATb[ai]], writes=[c.bankb[BK_O]])
                P.add("pe", lambda e: e.matmul(op_[:, 0:64], lhsT=S[s0], rhs=QBl[u][:, blk * 128: blk * 128 + 64], start=False, stop=False),
                      reads=[Sb[s0], ub_[u]], writes=[c.bankb[BK_O]])
                P.add("pe", lambda e: e.matmul(op_[:, 64:128], lhsT=S[s1], rhs=QBl[u][:, blk * 128 + 64: blk * 128 + 128], start=False, stop=True),
                      reads=[Sb[s1], ub_[u]], writes=[c.bankb[BK_O]])
                P.add("act", lambda e: e.activation(out=c.OL[:, h, t * TT + blk * 128: t * TT + (blk + 1) * 128], in_=op_, func=AF.Copy),
                      reads=[c.bankb[BK_O]], writes=[c.OLb[h][t]] + ([NSQb, NRSb] if (NSQb is not None and h >= 2) else []))
            return f
        return [mk1(b) for b in range(4)], [mk2(b) for b in range(4)]

    NU = 4 * NT
    for f in front_segs(0):
        f()
    for ui in range(NU):
        p1, p2 = back_blocks(ui)
        ff = front_segs(ui + 1) if ui + 1 < NU else [None] * 4
        p1[0]()
        for i in range(4):
            if ff[i] is not None:
                ff[i]()
            if i + 1 < 4:
                p1[i + 1]()
            p2[i]()


def emit_m2(c, w_in, pool_w, w_pa, w_pb, w_out, uall_d, dall_d, halo_d, ps_col=24, hn_col=28, cc_out=None, ccoutb=None):
    P = c.P
    a = c.arena
    winv = w_in.rearrange("(k p) n -> p k n", p=128)
    s = c.S0

    def take(n):
        nonlocal s
        r = a[:, s:s + n]
        s += n
        return r
    PO_BASE = 48128 - 4096
    s = c.S0
    PO = _bf(a[:, PO_BASE:PO_BASE + 4096]).rearrange("p (g t) -> p g t", g=4)
    POb = [[Buf() for t in range(NT)] for g in range(4)]
    olall = []
    WP = [_bf(take(512)).rearrange("p (k n) -> p k n", k=8) for i in range(4)]
    WPb = [Buf() for i in range(4)]
    PW = [_bf(take(64)) for i in range(4)]
    PWb = [Buf() for i in range(4)]
    HALO = _bf(take(64)).rearrange("p (k j) -> p k j", k=8)
    halob = Buf()
    UB = [take(528) for i in range(2)]
    UBb = [Buf() for i in range(2)]
    PP = [take(528) for i in range(2)]
    PPb = [Buf() for i in range(2)]
    DIFF = [_bf(take(256)) for i in range(2)]
    dfb = [Buf() for i in range(2)]
    P.barrier(skip=("dma_cc",))
    HT = take(192)
    htb = Buf()

    def load_halo():
        ccv = cc_out.rearrange("(r p) n -> p r n", p=128)
        P.add("sp", lambda e: e.dma_start(out=HT.rearrange("p (r n) -> p r n", r=3), in_=ccv[:, 0:3, 516:580]), reads=[ccoutb], writes=[htb], dma="x")
        hflat = HALO.rearrange("p k j -> p (k j)")
        htv = _bf(HT).rearrange("p (r n) -> p r n", r=3)
        P.add("dve", lambda e: e.tensor_scalar(out=hflat, in0=htv[:, 0, :], scalar1=c.COEF[:, 4:5], scalar2=None, op0=ALU.mult),
              reads=[htb, c.coefb], writes=[halob])
        for r_ in (1, 2):
            P.add("dve", lambda e, r_=r_: e.scalar_tensor_tensor(out=hflat, in0=htv[:, r_, :], scalar=c.COEF[:, 4 + r_:5 + r_], in1=hflat, op0=ALU.mult, op1=ALU.add),
                  reads=[htb, c.coefb, halob], writes=[halob])

    def load_pool(g):
        wload(c, WP[g], winv[:, :, g * 128:(g + 1) * 128], WPb[g], "g%d" % (g % 2))
        wload(c, PW[g], pool_w[g], PWb[g], "u%d" % (g % 2))
    for g_ in range(4):
        load_pool(g_)
    B_U, B_UH, B_MX = 0, 6, 2
    PP2 = [take(528) for i in range(2)]
    PP2b = [Buf() for i in range(2)]
    assert s <= PO_BASE, s
    bunits = [(g, t) for g in range(4) for t in range(1, NT)] + [(g, 0) for g in range(4)]

    def bA(ui):
        g, t = bunits[ui]
        u = ui % 2
        wb = g
        bu = B_U + u
        for k in range(8):
            P.add("pe", lambda e, k=k: e.matmul(c.banks[bu][:], lhsT=WP[wb][:, k, :], rhs=c.H[:, k, tl(t)], start=(k == 0), stop=(k == 7)),
                  reads=[WPb[wb], c.Hb[t]], writes=[c.bankb[bu]])
        if t == 0:
            for k in range(8):
                P.add("pe", lambda e, k=k: e.matmul(c.banks[B_UH][:, 0:16], lhsT=WP[wb][:, k, :], rhs=HALO[:, k, :], start=(k == 0), stop=(k == 7)),
                      reads=[WPb[wb], halob], writes=[c.bankb[B_UH]])
            P.add("dve", lambda e: e.tensor_scalar(out=UB[u][:, 0:16], in0=c.banks[B_UH][:, 0:16], scalar1=c.COEF[:, 3:4], scalar2=None, op0=ALU.mult),
                  reads=[c.bankb[B_UH], c.coefb], writes=[UBb[u]])
        else:
            for k in range(8):
                P.add("pe", lambda e, k=k: e.matmul(c.banks[B_UH][:, 0:16], lhsT=WP[wb][:, k, :], rhs=c.H[:, k, t * TT - 16:t * TT], start=(k == 0), stop=(k == 7)),
                      reads=[WPb[wb], c.Hb[t - 1]], writes=[c.bankb[B_UH]])
            P.add("act", lambda e: e.activation(out=UB[u][:, 0:16], in_=c.banks[B_UH][:, 0:16], func=AF.Copy), reads=[c.bankb[B_UH]], writes=[UBb[u]])
        P.add("act", lambda e: e.activation(out=UB[u][:, 16:528], in_=c.banks[bu][:], func=AF.Copy), reads=[c.bankb[bu]], writes=[UBb[u]])

    def bB(ui):
        g, t = bunits[ui]
        u = ui % 2
        wb = g
        w = 2 ** (g + 1)
        bmx = B_MX + u
        eng = "dve" if ui % 2 == 0 else "pool"
        pp, ppb = (PP, PPb) if eng == "dve" else (PP2, PP2b)
        src, srcb = UB[u], UBb[u]
        step, lo, pi = 1, 0, 0
        while step < w:
            dst, dstb = pp[pi % 2], ppb[pi % 2]
            pi += 1
            nlo = lo + step
            P.add(eng, lambda e, src=src, dst=dst, nlo=nlo, step=step: e.tensor_tensor(out=dst[:, nlo:528], in0=src[:, nlo:528], in1=src[:, nlo - step:528 - step], op=ALU.add),
                  reads=[srcb], writes=[dstb])
            src, srcb = dst, dstb
            lo = nlo
            step *= 2
        plf, plfb = src[:, 16:528], srcb
        P.add(eng, lambda e: e.tensor_scalar(out=plf, in0=plf, scalar1=1.0 / w, scalar2=0.0, op0=ALU.mult, op1=ALU.add), reads=[srcb], writes=[srcb])
        if t == 0:
            P.add(eng, lambda e: e.tensor_tensor(out=plf[:, 0:16], in0=plf[:, 0:16], in1=c.FIX[:, g * 16:(g + 1) * 16], op=ALU.mult),
                  reads=[plfb, c.coefb], writes=[plfb])
        P.add(eng, lambda e: e.tensor_tensor(out=DIFF[u], in0=plf, in1=UB[u][:, 16:528], op=ALU.subtract), reads=[plfb, UBb[u]], writes=[dfb[u]])
        P.add("pe", lambda e: e.matmul(c.banks[bmx][:], lhsT=PW[wb], rhs=DIFF[u], start=True, stop=True),
              reads=[PWb[wb], dfb[u]], writes=[c.bankb[bmx]])
        P.add("dve", lambda e: e.tensor_scalar(out=PO[:, g, tl(t)], in0=c.banks[bmx][:], scalar1=c.CON[:, ps_col + g:ps_col + 1 + g], scalar2=None, op0=ALU.mult),
              reads=[c.bankb[bmx], c.conb] + olall, writes=[POb[g][t]])

    bA(0)
    for ui in range(len(bunits)):
        if ui + 1 < len(bunits):
            if bunits[ui + 1] == (0, 0):
                load_halo()
            bA(ui + 1)
        bB(ui)
    s = c.S0
    UALL = take(1536).rearrange("p (i h v) -> p i h v", i=3, h=4)
    DALL = take(12).rearrange("p (i h) -> p i h", i=3)
    ualb = Buf("uall")
    SINf = take(512).rearrange("p (h v) -> p h v", h=4)
    SINb = _bf(take(256)).rearrange("p (h v) -> p h v", h=4)
    sinb = Buf("sin")
    WOG = [_bf(take(512)).rearrange("p (k n) -> p k n", k=8) for i in range(2)]
    WOGb = [Buf("wog") for i in range(2)]
    OS = [take(512) for i in range(2)]
    SQh = [_bf(take(256)) for i in range(2)]
    RSh = [take(512) for i in range(2)]
    SOG = [take(512) for i in range(2)]
    T1 = OS
    osb = [Buf() for i in range(2)]
    sqb = [Buf() for i in range(2)]
    rsb = [Buf() for i in range(2)]
    sogb = [Buf() for i in range(2)]
    t1b = osb
    assert s <= PO_BASE
    OGc = [a[:, c.S0 + i * 512: c.S0 + (i + 1) * 512] for i in range(2)]
    ogcb = [Buf() for i in range(2)]
    P.barrier()
    if cc_out is None:
        P.add("sp", lambda e: e.dma_start(out=UALL.rearrange("p i h v -> p (i h v)"), in_=uall_d), writes=[ualb], dma="x")
        P.add("sp", lambda e: e.dma_start(out=DALL.rearrange("p i h -> p (i h)"), in_=dall_d), writes=[ualb], dma="x")
    else:
        ccv = cc_out.rearrange("(r p) n -> p r n", p=128)
        P.add("sp", lambda e: e.dma_start(out=UALL.rearrange("p i h v -> p i (h v)"), in_=ccv[:, 0:3, 0:512]), reads=[ccoutb], writes=[ualb], dma="x")
        P.add("sp", lambda e: e.dma_start(out=DALL, in_=ccv[:, 0:3, 512:516]), reads=[ccoutb], writes=[ualb], dma="x")

    def load_og(h):
        wload(c, WOG[h % 2], winv[:, :, 2048 + h * 128: 2048 + (h + 1) * 128], WOGb[h % 2], "g%d" % (h % 2))
    load_og(0)
    for i in (1, 2):
        for h in range(4):
            P.add("dve", lambda e, i=i, h=h: e.scalar_tensor_tensor(out=UALL[:, i, h, :], in0=UALL[:, i - 1, h, :], scalar=DALL[:, i, h:h + 1],
                                                                  in1=UALL[:, i, h, :], op0=ALU.mult, op1=ALU.add),
                  reads=[ualb], writes=[ualb])
    sf = SINf.rearrange("p h v -> p (h v)")
    P.add("dve", lambda e: e.tensor_scalar(out=sf, in0=UALL[:, 0].rearrange("p h v -> p (h v)"), scalar1=c.COEF[:, 0:1], scalar2=None, op0=ALU.mult),
          reads=[ualb, c.coefb], writes=[sinb])
    for i in (1, 2):
        P.add("dve", lambda e, i=i: e.scalar_tensor_tensor(out=sf, in0=UALL[:, i].rearrange("p h v -> p (h v)"), scalar=c.COEF[:, i:i + 1], in1=sf,
                                                          op0=ALU.mult, op1=ALU.add),
              reads=[ualb, c.coefb, sinb], writes=[sinb])
    P.add("dve", lambda e: e.tensor_copy(out=SINb.rearrange("p h v -> p (h v)"), in_=sf), reads=[sinb], writes=[sinb])
    B_OC, B_OG, B_MS = 0, 1, 2
    units = [(h, t) for h in range(4) for t in range(NT)]

    def stageA(ui):
        h, t = units[ui]
        u = ui % 2
        wb = h % 2
        boc, bog, bms = B_OC + 3 * u, B_OG + 3 * u, B_MS + 3 * u
        if t == 0 and h + 1 < 4:
            load_og(h + 1)
        P.add("pe", lambda e: e.matmul(c.banks[boc][:], lhsT=SINb[:, h, :], rhs=c.QB[:, h, tl(t)], start=True, stop=True),
              reads=[sinb, c.QBb[h][t]], writes=[c.bankb[boc]])
        for k in range(8):
            P.add("pe", lambda e, k=k: e.matmul(c.banks[bog][:], lhsT=WOG[wb][:, k, :], rhs=c.H[:, k, tl(t)], start=(k == 0), stop=(k == 7)),
                  reads=[WOGb[wb], c.Hb[t]], writes=[c.bankb[bog]])
        P.add("dve", lambda e: e.tensor_tensor(out=OS[u], in0=c.banks[boc][:], in1=c.OL[:, h, tl(t)], op=ALU.add),
              reads=[c.bankb[boc], c.OLb[h][t]], writes=[osb[u]])
        P.add("act", lambda e: e.activation(out=SQh[u], in_=OS[u], func=AF.Square), reads=[osb[u]], writes=[sqb[u]])
        P.add("pe", lambda e: e.matmul(c.banks[bms][:], lhsT=c.ONES, rhs=SQh[u], start=True, stop=True),
              reads=[sqb[u], c.constb], writes=[c.bankb[bms]])
        P.add("act", lambda e: e.activation(out=SOG[u], in_=c.banks[bog][:], func=AF.Exp, scale=-1.0), reads=[c.bankb[bog]], writes=[sogb[u]])
        P.add("act", lambda e: e.activation(out=OGc[u], in_=c.banks[bog][:], func=AF.Copy), reads=[c.bankb[bog]], writes=[ogcb[u], ualb])

    def stageB(ui):
        h, t = units[ui]
        u = ui % 2
        boc, bog, bms = B_OC + 3 * u, B_OG + 3 * u, B_MS + 3 * u
        P.add("act", lambda e: e.activation(out=RSh[u], in_=c.banks[bms][:], func=AF.Ln, scale=1.0 / 128, bias=c.EPSC),
              reads=[c.bankb[bms], c.constb], writes=[rsb[u]])
        P.add("act", lambda e: e.activation(out=RSh[u], in_=RSh[u], func=AF.Exp, scale=-0.5), reads=[rsb[u]], writes=[rsb[u]])
        P.add("act", lambda e: e.activation(out=SOG[u], in_=SOG[u], func=AF.Ln, scale=1.0, bias=c.ONEC), reads=[sogb[u], c.constb], writes=[sogb[u]])
        P.add("act", lambda e: e.activation(out=SOG[u], in_=SOG[u], func=AF.Exp, scale=-1.0), reads=[sogb[u]], writes=[sogb[u]])
        P.add("dve", lambda e: e.tensor_tensor(out=SOG[u], in0=OGc[u], in1=SOG[u], op=ALU.mult), reads=[ogcb[u], sogb[u]], writes=[sogb[u]])
        P.add("dve", lambda e: e.scalar_tensor_tensor(out=T1[u], in0=OS[u], scalar=c.CON[:, hn_col + h:hn_col + 1 + h], in1=RSh[u], op0=ALU.mult, op1=ALU.mult),
              reads=[osb[u], rsb[u], c.conb], writes=[t1b[u], osb[u]])
        P.add("pool", lambda e: e.tensor_tensor(out=c.QB[:, h, tl(t)], in0=T1[u], in1=SOG[u], op=ALU.mult),
              reads=[t1b[u], sogb[u]], writes=[c.QBb[h][t]])

    stageA(0)
    for ui in range(len(units)):
        if ui + 1 < len(units):
            stageA(ui + 1)
        stageB(ui)
    HGO, HGOb = c.QB, c.QBb
    s = c.S0
    Mall = _bf(a[:, 24576:32768]).rearrange("p (n t) -> p n t", n=8)
    olall = [c.OLb[h][t] for h in range(4) for t in range(NT)]

    def Mv(n, t):
        return Mall[:, n, tl(t)]
    Mb = [[Buf() for t in range(NT)] for n in range(8)]
    WGA = [_bf(take(512)).rearrange("p (k n) -> p k n", k=8) for i in range(2)]
    WGB = [_bf(take(512)).rearrange("p (k n) -> p k n", k=8) for i in range(2)]
    WPA = [_bf(take(256)).rearrange("p (g n) -> p g n", g=4) for i in range(2)]
    WPB = [_bf(take(256)).rearrange("p (g n) -> p g n", g=4) for i in range(2)]
    wcb = [Buf() for i in range(2)]
    SGA = [_bf(take(256)) for i in range(2)]
    SGB = [_bf(take(256)) for i in range(2)]
    TA = [take(512) for i in range(2)]
    TB = [take(512) for i in range(2)]
    sgab = [Buf() for i in range(2)]
    sgbb = [Buf() for i in range(2)]
    tab = [Buf() for i in range(2)]
    tbb = [Buf() for i in range(2)]
    assert s <= PO_BASE
    wpav = w_pa.rearrange("(g p) n -> p g n", p=128)
    wpbv = w_pb.rearrange("(g p) n -> p g n", p=128)

    def load_c(n):
        b = n % 2
        wload(c, WGA[b], winv[:, :, 2560 + n * 128: 2560 + (n + 1) * 128], wcb[b], "g%d" % b)
        wload(c, WGB[b], winv[:, :, 3584 + n * 128: 3584 + (n + 1) * 128], wcb[b], "u%d" % b)
        wload(c, WPA[b], wpav[:, :, n * 128:(n + 1) * 128], wcb[b], "d%d" % b)
        wload(c, WPB[b], wpbv[:, :, n * 128:(n + 1) * 128], wcb[b], "e%d" % b)
    P.barrier()
    load_c(0)
    ui = 0
    for n in range(8):
        if n + 1 < 8:
            load_c(n + 1)
        wb = n % 2
        for t in range(NT):
            u = ui % 2
            ui += 1
            bga, bgb, bpa, bpb = 0 + 4 * u, 1 + 4 * u, 2 + 4 * u, 3 + 4 * u
            for k in range(8):
                P.add("pe", lambda e, k=k, t=t, bga=bga: e.matmul(c.banks[bga][:], lhsT=WGA[wb][:, k, :], rhs=c.H[:, k, tl(t)], start=(k == 0), stop=(k == 7)),
                      reads=[wcb[wb], c.Hb[t]], writes=[c.bankb[bga]])
            for k in range(8):
                P.add("pe", lambda e, k=k, t=t, bgb=bgb: e.matmul(c.banks[bgb][:], lhsT=WGB[wb][:, k, :], rhs=c.H[:, k, tl(t)], start=(k == 0), stop=(k == 7)),
                      reads=[wcb[wb], c.Hb[t]], writes=[c.bankb[bgb]])
            for g in range(4):
                P.add("pe", lambda e, g=g, t=t, bpa=bpa: e.matmul(c.banks[bpa][:], lhsT=WPA[wb][:, g, :], rhs=PO[:, g, tl(t)], start=(g == 0), stop=(g == 3)),
                      reads=[wcb[wb], POb[g][t]], writes=[c.bankb[bpa]])
            for g in range(4):
                P.add("pe", lambda e, g=g, t=t, bpb=bpb: e.matmul(c.banks[bpb][:], lhsT=WPB[wb][:, g, :], rhs=HGO[:, g, tl(t)], start=(g == 0), stop=(g == 3)),
                      reads=[wcb[wb], HGOb[g][t]], writes=[c.bankb[bpb]])
            P.add("act", lambda e, u=u, bga=bga: e.activation(out=SGA[u], in_=c.banks[bga][:], func=AF.Sigmoid), reads=[c.bankb[bga]], writes=[sgab[u]])
            P.add("act", lambda e, u=u, bgb=bgb: e.activation(out=SGB[u], in_=c.banks[bgb][:], func=AF.Sigmoid), reads=[c.bankb[bgb]], writes=[sgbb[u]])
            P.add("dve", lambda e, u=u, bpa=bpa: e.tensor_tensor(out=TA[u], in0=c.banks[bpa][:], in1=SGA[u], op=ALU.mult), reads=[c.bankb[bpa], sgab[u]], writes=[tab[u]])
            P.add("dve", lambda e, u=u, bpb=bpb: e.tensor_tensor(out=TB[u], in0=c.banks[bpb][:], in1=SGB[u], op=ALU.mult), reads=[c.bankb[bpb], sgbb[u]], writes=[tbb[u]])
            P.add("pool", lambda e, u=u, n=n, t=t: e.tensor_tensor(out=Mv(n, t), in0=TA[u], in1=TB[u], op=ALU.add),
                  reads=[tab[u], tbb[u]] + olall, writes=[Mb[n][t]])
    WO = [_bf(take(512)).rearrange("p (k n) -> p k n", k=8) for i in range(2)]
    WOb = [Buf() for i in range(2)]
    assert s <= PO_BASE
    wov = w_out.rearrange("(k p) n -> p k n", p=128)

    def load_o(n):
        wload(c, WO[n % 2], wov[:, :, n * 128:(n + 1) * 128], WOb[n % 2], "g%d" % (n % 2))
    load_o(0)
    ui = 0
    for n in range(8):
        if n + 1 < 8:
            load_o(n + 1)
        wb = n % 2
        for t in range(NT):
            bo = ui % 2
            ui += 1
            for k in range(8):
                P.add("pe", lambda e, k=k, t=t, bo=bo: e.matmul(c.banks[bo][:], lhsT=WO[wb][:, k, :], rhs=Mv(k, t), start=(k == 0), stop=(k == 7)),
                      reads=[WOb[wb], Mb[k][t]], writes=[c.bankb[bo]])
            P.add("dve", lambda e, n=n, t=t, bo=bo: e.tensor_tensor(out=c.X[:, n, tl(t)], in0=c.banks[bo][:], in1=c.X[:, n, tl(t)], op=ALU.add),
                  reads=[c.bankb[bo], c.Xb[n][t]], writes=[c.Xb[n][t]])


def emit_final_norm(c, out_d, gcol0=16):
    P = c.P
    a = c.arena
    s = c.S0
    SQ = _bf(a[:, s:s + 2048]).rearrange("p (k n) -> p k n", k=8)
    RS = a[:, s + 2048:s + 2560]
    OB = [a[:, s + 2560 + i * 4096: s + 2560 + (i + 1) * 4096].rearrange("p (k n) -> p k n", k=8) for i in range(2)]
    SQb, RSb = Buf(), Buf()
    OBb = [Buf(), Buf()]
    outv = out_d.rearrange("(k p) t -> p k t", p=128)
    stores = []

    def dst(t, rs, rsb):
        ob = OB[t % 2]
        for k in range(8):
            P.add("dve", lambda e, k=k, t=t, ob=ob: e.scalar_tensor_tensor(out=ob[:, k, :], in0=c.X[:, k, tl(t)], scalar=c.CON[:, gcol0 + k:gcol0 + 1 + k], in1=rs,
                                                                         op0=ALU.mult, op1=ALU.mult),
                  reads=[c.Xb[k][t], rsb, c.conb], writes=[OBb[t % 2]])
        stores.append(P.add("sp", lambda e, t=t, ob=ob: e.dma_start(out=outv[:, :, tl(t)], in_=ob), reads=[OBb[t % 2]], dma="o"))
    P.barrier()
    emit_norm(c, gcol0, SQ, SQb, RS, RSb, dst_fn=dst)
    return stores


def load_small(c, con_d, coef_d, fix_d):
    P = c.P
    P.add("sp", lambda e: e.dma_start(out=c.CON, in_=con_d), writes=[c.conb], dma="x")
    P.add("sp", lambda e: e.dma_start(out=c.COEF, in_=coef_d), writes=[c.coefb], dma="x")
    P.add("sp", lambda e: e.dma_start(out=c.FIX, in_=fix_d), writes=[c.coefb], dma="x")


def all_state_bufs(c):
    r = [c.Xb[n][t] for n in range(8) for t in range(NT)] + list(c.Hb)
    r += [c.OLb[h][t] for h in range(4) for t in range(NT)] + [c.QBb[h][t] for h in range(4) for t in range(NT)]
    return r


def build_A():
    nc = bass.Bass("TRN2", target_bir_lowering=False)
    xT = nc.dram_tensor("xT", [D, T], F32, kind="ExternalInput").ap()
    con = nc.dram_tensor("con", [128, 64], F32, kind="ExternalInput").ap()
    coef = nc.dram_tensor("coef", [128, 8], F32, kind="ExternalInput").ap()
    fix = nc.dram_tensor("fix", [128, 64], F32, kind="ExternalInput").ap()
    wg = nc.dram_tensor("wg", [D, DFF], F32, kind="ExternalInput").ap()
    wu = nc.dram_tensor("wu", [D, DFF], F32, kind="ExternalInput").ap()
    wd = nc.dram_tensor("wd", [DFF, D], F32, kind="ExternalInput").ap()
    w_in = nc.dram_tensor("w_in", [D, DIN], F32, kind="ExternalInput").ap()
    state = nc.dram_tensor("state", [128, STATE_WORDS], F32, kind="ExternalOutput").ap()
    uout = nc.dram_tensor("uout", [128, 512], F32, kind="ExternalOutput").ap()
    dout = nc.dram_tensor("dout", [128, 4], F32, kind="ExternalOutput").ap()
    with contextlib.ExitStack() as st:
        P = Prog(nc)
        c = setup_common(nc, P, st)
        load_small(c, con, coef, fix)
        xv = xT.rearrange("(k p) t -> p k t", p=128)
        for k in range(8):
            P.add("sp", lambda e, k=k: e.dma_start(out=c.X[:, k, :], in_=xv[:, k, :]), writes=[c.Xb[k][t] for t in range(NT)], dma="x")
        emit_consts(c)
        if "lb" in STAGES:
            emit_lb(c)
        if "ffn" in STAGES:
            emit_ffn(c, wg, wu, wd, 0)
        if "normmix" in STAGES:
            emit_norm_mix(c)
        if "m1" in STAGES:
            emit_m1(c, w_in)
        outs = []
        sb = all_state_bufs(c)
        for i in range(9):
            outs.append(P.add("sp", lambda e, i=i: e.dma_start(out=state[:, i * 4096:(i + 1) * 4096], in_=c.arena[:, i * 4096:(i + 1) * 4096]), reads=sb, dma="o"))
        outs.append(P.add("sp", lambda e: e.dma_start(out=uout, in_=c.UOUT.rearrange("p h v -> p (h v)")), reads=[c.uoutb], dma="o"))
        outs.append(P.add("sp", lambda e: e.dma_start(out=dout, in_=c.DOUT), reads=[c.uoutb], dma="o"))
        P.add("sp", None, deps=outs)
        P.emit()
    return nc


def emit_norm_mix(c, gcol0=8):
    a = c.arena
    s = c.S0
    SQ = _bf(a[:, s:s + 2048]).rearrange("p (k n) -> p k n", k=8)
    RS = a[:, s + 2048:s + 2560]
    c.P.barrier()
    emit_norm(c, gcol0, SQ, Buf(), RS, Buf())


def build_B():
    nc = bass.Bass("TRN2", target_bir_lowering=False)
    state = nc.dram_tensor("state", [128, STATE_WORDS], F32, kind="ExternalInput").ap()
    con = nc.dram_tensor("con", [128, 64], F32, kind="ExternalInput").ap()
    coef = nc.dram_tensor("coef", [128, 8], F32, kind="ExternalInput").ap()
    fix = nc.dram_tensor("fix", [128, 64], F32, kind="ExternalInput").ap()
    uall = nc.dram_tensor("uall", [128, 1536], F32, kind="ExternalInput").ap()
    dall = nc.dram_tensor("dall", [128, 12], F32, kind="ExternalInput").ap()
    halo = nc.dram_tensor("halo", [128, 64], F32, kind="ExternalInput").ap()
    w_in = nc.dram_tensor("w_in", [D, DIN], F32, kind="ExternalInput").ap()
    pool_w = nc.dram_tensor("pool_w", [4, 128, 128], F32, kind="ExternalInput").ap()
    w_pa = nc.dram_tensor("w_pa", [512, D], F32, kind="ExternalInput").ap()
    w_pb = nc.dram_tensor("w_pb", [512, D], F32, kind="ExternalInput").ap()
    w_out = nc.dram_tensor("w_out", [D, D], F32, kind="ExternalInput").ap()
    wg = nc.dram_tensor("wg", [D, DFF], F32, kind="ExternalInput").ap()
    wu = nc.dram_tensor("wu", [D, DFF], F32, kind="ExternalInput").ap()
    wd = nc.dram_tensor("wd", [DFF, D], F32, kind="ExternalInput").ap()
    xo = nc.dram_tensor("xo", [D, T], F32, kind="ExternalOutput").ap()
    fo = nc.dram_tensor("fo", [D, T], F32, kind="ExternalOutput").ap()
    with contextlib.ExitStack() as st:
        P = Prog(nc)
        c = setup_common(nc, P, st)
        load_small(c, con, coef, fix)
        sb = all_state_bufs(c)
        for i in range(9):
            P.add("sp", lambda e, i=i: e.dma_start(out=c.arena[:, i * 4096:(i + 1) * 4096], in_=state[:, i * 4096:(i + 1) * 4096]), writes=sb if i == 8 else [], dma="x")
        emit_consts(c)
        if "m2" in BSTAGES:
            emit_m2(c, w_in, pool_w, w_pa, w_pb, w_out, uall, dall, halo)
        if "ffn" in BSTAGES:
            emit_ffn(c, wg, wu, wd, 0)
        outs = []
        xov = xo.rearrange("(k p) t -> p k t", p=128)
        for k in range(8):
            outs.append(P.add("sp", lambda e, k=k: e.dma_start(out=xov[:, k, :], in_=c.X[:, k, :]), reads=[c.Xb[k][t] for t in range(NT)], dma="o"))
        outs += emit_final_norm(c, fo)
        P.add("sp", None, deps=outs)
        P.emit()
    return nc


WNAMES = [("ffn1_w_gate", [2, D, DFF]), ("ffn1_w_up", [2, D, DFF]), ("ffn1_w_down", [2, DFF, D]), ("w_in", [2, D, DIN]),
          ("pool_w", [2, 4, 128, 128]), ("w_pool_proj", [2, 512, D]), ("w_hgrn_proj", [2, 512, D]), ("w_out", [2, D, D]),
          ("ffn2_w_gate", [2, D, DFF]), ("ffn2_w_up", [2, D, DFF]), ("ffn2_w_down", [2, DFF, D])]


def build_fused():
    nc = bass.Bass("TRN2", target_bir_lowering=False)
    xT = nc.dram_tensor("xT", [D, T], F32, kind="ExternalInput").ap()
    con = nc.dram_tensor("con", [128, 128], F32, kind="ExternalInput").ap()
    coef = nc.dram_tensor("coef", [128, 8], F32, kind="ExternalInput").ap()
    fix = nc.dram_tensor("fix", [128, 64], F32, kind="ExternalInput").ap()
    W = {n: nc.dram_tensor(n, shp, F32, kind="ExternalInput").ap() for n, shp in WNAMES}
    out_d = nc.dram_tensor("out", [D, T], F32, kind="ExternalOutput").ap()
    cc_in = [nc.dram_tensor("cc_in%d" % l, [128, 580], F32, kind="Internal").ap() for l in range(2)]
    cc_out = [nc.dram_tensor("cc_out%d" % l, [512, 580], F32, kind="Internal", addr_space="Local").ap() for l in range(2)]
    with contextlib.ExitStack() as st:
        P = Prog(nc)
        c = setup_common(nc, P, st)
        cb = 48128
        c.CON = c.arena[:, cb + 416:cb + 544]
        load_small(c, con, coef, fix)
        xv = xT.rearrange("(k p) t -> p k t", p=128)
        for k in range(8):
            P.add("sp", lambda e, k=k: e.dma_start(out=c.X[:, k, :], in_=xv[:, k, :]), writes=[c.Xb[k][t] for t in range(NT)], dma="x")
        emit_consts(c)
        for l in range(2):
            base = l * 40
            emit_lb(c, 88, 92, layer=l)
            emit_ffn(c, W["ffn1_w_gate"][l], W["ffn1_w_up"][l], W["ffn1_w_down"][l], base + 0)
            ccinb, ccoutb = Buf(), Buf()

            def final_fn(h, S_ap, S_buf, l=l, ccinb=ccinb):
                P.add("sp", lambda e: e.dma_start(out=cc_in[l][:, h * 128:(h + 1) * 128], in_=S_ap), reads=[S_buf], writes=[ccinb], dma="y")
            emit_m1(c, W["w_in"][l], final_fn=final_fn, norm_gcol=base + 8)
            P.add("sp", lambda e, l=l: e.dma_start(out=cc_in[l][:, 512:516], in_=c.DOUT), reads=[c.uoutb], writes=[ccinb], dma="y")
            hw = c.arena[:, 16384:24576].rearrange("p (k w) -> p k w", k=8)[:, :, 1016:1024]
            P.add("sp", lambda e, l=l: e.dma_start(out=cc_in[l][:, 516:580].rearrange("p (k w) -> p k w", k=8), in_=hw), reads=list(c.Hb), writes=[ccinb], dma="y")
            P.add("pool", lambda e, l=l: e.collective_compute("AllGather", ALU.bypass, replica_groups=[[0, 1, 2, 3], [4, 5, 6, 7]],
                                                              ins=[cc_in[l]], outs=[cc_out[l]]),
                  reads=[ccinb], writes=[ccoutb], dma="cc")
            emit_m2(c, W["w_in"][l], W["pool_w"][l], W["w_pool_proj"][l], W["w_hgrn_proj"][l], W["w_out"][l], None, None, None,
                    ps_col=base + 24, hn_col=base + 28, cc_out=cc_out[l], ccoutb=ccoutb)
            emit_ffn(c, W["ffn2_w_gate"][l], W["ffn2_w_up"][l], W["ffn2_w_down"][l], base + 16)
        outs = emit_final_norm(c, out_d, gcol0=80)
        P.add("sp", None, deps=outs)
        P.emit()
    return nc


def _pack_con_fused(I):
    con = np.zeros((128, 128), np.float32)
    for l in range(2):
        b = l * 40
        con[:, b + 0:b + 8] = _cols(I["ffn1_norm"][l], 8)
        con[:, b + 8:b + 16] = _cols(I["mix_norm"][l], 8)
        con[:, b + 16:b + 24] = _cols(I["ffn2_norm"][l], 8)
        con[:, b + 24:b + 28] = _cols(I["pool_scale"][l], 4)
        con[:, b + 28:b + 32] = _cols(I["hgrn_norm"][l], 4)
    con[:, 80:88] = _cols(I["final_norm"], 8)
    con[:, 88:92] = _cols(I["lb_logits"][0], 4)
    con[:, 92:96] = _cols(I["lb_logits"][1], 4)
    return con


def _core_cfg_fused(core):
    pos = core % 4
    coef = np.zeros((128, 8), np.float32)
    coef[:, 3] = 1.0
    if pos >= 1:
        coef[:, pos - 1] = 1.0
        coef[:, 4 + pos - 1] = 1.0
    return coef, _core_cfg(core)[1]


def kernel(**I):
    f = lambda v: np.ascontiguousarray(np.asarray(v, dtype=np.float32))
    x = f(I["x"])
    cores = list(range(NCORES))
    con = _pack_con_fused(I)
    Wd = {n: f(I[n]) for n, _ in WNAMES}
    in_maps = []
    for i in cores:
        coef, fix = _core_cfg_fused(i)
        xs = np.ascontiguousarray(x[i // 4, (i % 4) * T:(i % 4 + 1) * T, :].T)
        in_maps.append(dict(xT=xs, con=con, coef=coef, fix=fix, **Wd))
    if "F" not in _PROGS:
        _PROGS["F"] = build_fused()
    res = run_bass_kernel_spmd(_PROGS["F"], in_maps, core_ids=cores).results
    out = np.empty((2, 8192, D), np.float32)
    for i in cores:
        out[i // 4, (i % 4) * T:(i % 4 + 1) * T, :] = res[i]["out"].T
    return out


def _cols(v, n):
    return np.ascontiguousarray(np.asarray(v, np.float32).reshape(n, 128).T)


def _pack_con(ffn_norm, mix_norm, final_norm, pool_scale, hgrn_norm, lb0, lb1, lb_on):
    con = np.zeros((128, 64), np.float32)
    con[:, 0:8] = _cols(ffn_norm, 8)
    con[:, 8:16] = _cols(mix_norm, 8)
    con[:, 16:24] = _cols(final_norm, 8)
    con[:, 24:28] = _cols(pool_scale, 4)
    con[:, 28:32] = _cols(hgrn_norm, 4)
    con[:, 32:36] = _cols(lb0, 4)
    con[:, 36:40] = _cols(lb1, 4)
    con[:, 40] = lb_on
    return con


def _core_cfg(core):
    pos = core % 4
    coef = np.zeros((128, 8), np.float32)
    if pos >= 1:
        coef[:, pos - 1] = 1.0
        coef[:, 3] = 1.0
    fix = np.ones((128, 4, 16), np.float32)
    if pos == 0:
        for g in range(4):
            w = 2 ** (g + 1)
            for t in range(16):
                fix[:, g, t] = np.float32(w) / np.float32(min(t + 1, w))
    return coef, fix.reshape(128, 64)


_PROGS = {}
_DBG = None


def _prog(name):
    if name not in _PROGS:
        _PROGS[name] = {"A": build_A, "B": build_B}[name]()
    return _PROGS[name]


def kernel_unfused(x, ffn1_norm, ffn1_w_gate, ffn1_w_up, ffn1_w_down, mix_norm, w_in, pool_w,
           pool_scale, lb_logits, hgrn_norm, w_pool_proj, w_hgrn_proj, w_out,
           ffn2_norm, ffn2_w_gate, ffn2_w_up, ffn2_w_down, final_norm):
    f = lambda v: np.ascontiguousarray(np.asarray(v, dtype=np.float32))
    x = f(x)
    cores = list(range(NCORES))
    xs = [np.ascontiguousarray(x[cidx // 4, (cidx % 4) * T:(cidx % 4 + 1) * T, :].T) for cidx in cores]
    cfg = [_core_cfg(cidx) for cidx in cores]
    L = 2
    final = None
    for l in range(L):
        conA = _pack_con(ffn1_norm[l], mix_norm[l], final_norm, pool_scale[l], hgrn_norm[l], lb_logits[0], lb_logits[1], 0.0 if l == 0 else 1.0)
        conB = _pack_con(ffn2_norm[l], mix_norm[l], final_norm, pool_scale[l], hgrn_norm[l], lb_logits[0], lb_logits[1], 0.0 if l == 0 else 1.0)
        wA = {"wg": f(ffn1_w_gate[l]), "wu": f(ffn1_w_up[l]), "wd": f(ffn1_w_down[l]), "w_in": f(w_in[l])}
        inA = [dict(xT=xs[i], con=conA, coef=cfg[i][0], fix=cfg[i][1], **wA) for i in cores]
        rA = run_bass_kernel_spmd(_prog("A"), inA, core_ids=cores).results
        inB = []
        wB = {"w_in": wA["w_in"], "pool_w": f(pool_w[l]), "w_pa": f(w_pool_proj[l]), "w_pb": f(w_hgrn_proj[l]), "w_out": f(w_out[l]),
              "wg": f(ffn2_w_gate[l]), "wu": f(ffn2_w_up[l]), "wd": f(ffn2_w_down[l])}
        for i in cores:
            g0 = (i // 4) * 4
            uall = np.ascontiguousarray(np.concatenate([rA[g0 + j]["uout"] for j in range(3)], axis=1))
            dall = np.ascontiguousarray(np.concatenate([rA[g0 + j]["dout"] for j in range(3)], axis=1))
            prev = rA[(i - 1) % NCORES]["state"]
            hreg = prev[:, 16384:24576].reshape(128, 8, 1024)
            halo = np.ascontiguousarray(hreg[:, :, 1016:1024].reshape(128, 64))
            inB.append(dict(state=rA[i]["state"], con=conB, coef=cfg[i][0], fix=cfg[i][1], uall=uall, dall=dall, halo=halo, **wB))
        rB = run_bass_kernel_spmd(_prog("B"), inB, core_ids=cores).results
        xs = [rB[i]["xo"] for i in cores]
        final = [rB[i]["fo"] for i in cores]
        if _DBG is not None:
            _DBG["x_l%d" % l] = xs
    out = np.empty((2, 8192, D), np.float32)
    for i in cores:
        out[i // 4, (i % 4) * T:(i % 4 + 1) * T, :] = final[i].T
    return out
```

```python
import contextlib
import numpy as np
import concourse.bass as bass
import concourse.mybir as mybir
from concourse.bass_utils import run_bass_kernel_spmd

F32 = mybir.dt.float32
BF16 = mybir.dt.bfloat16
AF = mybir.ActivationFunctionType
ALU = mybir.AluOpType
ENGS = ["pe", "act", "dve", "pool", "sp"]
ENGOBJ = {"pe": "tensor", "act": "scalar", "dve": "vector", "pool": "gpsimd", "sp": "sync"}

D = 1024
DFF = 2816
DIN = 4608
T = 2048
TT = 512
NT = T // TT
NCORES = 8
EPS = 1e-6
FFN_GROUPS = [(0, 4), (4, 4), (8, 4), (12, 4), (16, 4), (20, 2)]
STAGES = {"lb", "ffn", "normmix", "m1"}
BSTAGES = {"m2", "ffn"}
STATE_WORDS = 36864


class Op:
    __slots__ = ("eng", "idx", "fn", "deps", "dma", "signaled", "val", "semkey")


class Buf:
    __slots__ = ("name", "writers", "readers")

    def __init__(self, name=""):
        self.name = name
        self.writers = {}
        self.readers = {}


class _Rec:
    def __init__(self):
        self.call = None

    def __getattr__(self, name):
        def f(*args, **kwargs):
            self.call = (name, args, kwargs)
            return None
        return f


class Prog:
    def __init__(self, nc):
        self.nc = nc
        self.ops = {e: [] for e in ENGS}
        self.dma_keys = []
        self.last_dma = {}

    def barrier(self, skip=()):
        lasts = [v for k, v in self.last_dma.items() if k not in skip]
        for e in ENGS:
            for o in reversed(self.ops[e]):
                if o.fn is not None and not (o.dma is not None and ("dma_" + o.dma) in skip):
                    lasts.append(o)
                    break
        for e in ENGS:
            self.add(e, None, deps=lasts, force_deps=True)

    def add(self, eng, fn, reads=(), writes=(), deps=(), dma=None, force_deps=False):
        op = Op()
        op.eng = eng
        op.idx = len(self.ops[eng])
        if fn is not None:
            rec = _Rec()
            fn(rec)
            assert rec.call is not None
            fn = rec.call
        op.fn = fn
        op.dma = dma
        op.signaled = False
        op.val = None
        dl = [d for d in deps if d is not None]
        if dma is not None and ("dma_" + dma) in self.last_dma:
            dl.append(self.last_dma["dma_" + dma])
        for b in reads:
            dl.extend(b.writers.values())
        for b in writes:
            dl.extend(b.writers.values())
            dl.extend(b.readers.values())
        fl = []
        for d in dl:
            if d is op:
                continue
            if d.dma is None and d.eng == eng and not force_deps:
                if eng == "pe":
                    continue
                if eng == "sp":
                    continue
                if op.idx - d.idx > 2:
                    continue
            fl.append(d)
        op.deps = fl
        if dma is not None:
            op.semkey = "dma_" + dma
            if op.semkey not in self.dma_keys:
                self.dma_keys.append(op.semkey)
            self.last_dma[op.semkey] = op
        else:
            op.semkey = "eng_" + eng
        self.ops[eng].append(op)
        for b in reads:
            b.readers[eng if dma is None else "dma_" + dma] = op
        for b in writes:
            b.writers[eng if dma is None else "dma_" + dma] = op
            b.readers = {}
        return op

    def emit(self):
        nc = self.nc
        for e in ENGS:
            for op in self.ops[e]:
                for d in op.deps:
                    d.signaled = True
                if op.dma is not None:
                    op.signaled = True
        counters = {}
        for e in ENGS:
            for op in self.ops[e]:
                if op.signaled:
                    inc = 16 if (op.dma is not None and not op.dma.startswith("cc")) else 1
                    counters[op.semkey] = counters.get(op.semkey, 0) + inc
                    op.val = counters[op.semkey]
        keys = ["eng_" + e for e in ENGS if any(o.signaled and o.dma is None for o in self.ops[e])]
        keys += self.dma_keys
        with contextlib.ExitStack() as st:
            sems = {k: st.enter_context(nc.semaphore(k)) for k in keys}
            block = st.enter_context(nc.Block())

            def make_body(e):
                def body(eng):
                    waited = {}
                    for op in self.ops[e]:
                        need = {}
                        for d in op.deps:
                            need[d.semkey] = max(need.get(d.semkey, 0), d.val)
                        for k, v in need.items():
                            if waited.get(k, 0) < v:
                                eng.wait_ge(sems[k], v)
                                waited[k] = v
                        if op.fn is None:
                            continue
                        name, args, kwargs = op.fn
                        ins = getattr(eng, name)(*args, **kwargs)
                        if op.signaled:
                            ins.then_inc(sems[op.semkey], 16 if (op.dma is not None and not op.dma.startswith("cc")) else 1)
                return body

            for e in ENGS:
                if self.ops[e]:
                    getattr(block, ENGOBJ[e])(make_body(e))


class Ctx:
    pass


def _bf(ap_f32):
    return ap_f32.bitcast(BF16)


def setup_common(nc, P, st):
    c = Ctx()
    c.nc = nc
    c.P = P
    arena = st.enter_context(nc.sbuf_tensor("arena", [128, 49152], F32))
    c.arena = arena
    c.banks = [st.enter_context(nc.psum_tensor("bank%d" % i, [128, 512], F32)) for i in range(8)]
    c.bankb = [Buf("bank%d" % i) for i in range(8)]
    c.X = arena[:, 0:16384].rearrange("p (c t) -> p c t", c=8)
    c.H = _bf(arena[:, 16384:24576]).rearrange("p (c t) -> p c t", c=8)
    c.OL = arena[:, 24576:32768].rearrange("p (h t) -> p h t", h=4)
    c.QB = _bf(arena[:, 32768:36864]).rearrange("p (h t) -> p h t", h=4)
    c.Xb = [[Buf("X%d_%d" % (n, t)) for t in range(NT)] for n in range(8)]
    c.Hb = [Buf("H%d" % t) for t in range(NT)]
    c.OLb = [[Buf("OL") for t in range(NT)] for h in range(4)]
    c.QBb = [[Buf("QB") for t in range(NT)] for h in range(4)]
    cb = 48128
    c.CON = arena[:, cb:cb + 64]
    c.COEF = arena[:, cb + 64:cb + 72]
    c.FIX = arena[:, cb + 72:cb + 136]
    c.ONES = _bf(arena[:, cb + 136:cb + 200])
    c.IDENT = _bf(arena[:, cb + 200:cb + 264])
    c.MASK = arena[:, cb + 264:cb + 392]
    c.EPSC = arena[:, cb + 392:cb + 393]
    c.ONEC = arena[:, cb + 393:cb + 394]
    c.LB = arena[:, cb + 400:cb + 404]
    c.OML = arena[:, cb + 404:cb + 408]
    c.NOML = arena[:, cb + 408:cb + 412]
    c.ZEROC = arena[:, cb + 412:cb + 413]
    c.UOUT = arena[:, cb + 416:cb + 928].rearrange("p (h v) -> p h v", h=4)
    c.DOUT = arena[:, cb + 928:cb + 932]
    c.conb = Buf("con")
    c.coefb = Buf("coef")
    c.constb = Buf("const")
    c.lbb = Buf("lb")
    c.uoutb = Buf("uout")
    c.S0 = 36864
    return c


def emit_consts(c):
    P = c.P
    P.add("dve", lambda e: e.memset(c.ONES, 1.0), writes=[c.constb])
    P.add("dve", lambda e: e.memset(c.IDENT, 0.0), writes=[c.constb])
    P.add("dve", lambda e: e.memset(c.MASK, 1.0), writes=[c.constb])
    P.add("dve", lambda e: e.memset(c.EPSC, EPS), writes=[c.constb])
    P.add("dve", lambda e: e.memset(c.ONEC, 1.0), writes=[c.constb])
    P.add("dve", lambda e: e.memset(c.ZEROC, 0.0), writes=[c.constb])
    P.add("pool", lambda e: e.affine_select(out=c.IDENT, in_=c.IDENT, pattern=[[-1, 128]], compare_op=ALU.not_equal,
                                            fill=1.0, base=0, channel_multiplier=1), reads=[c.constb], writes=[c.constb])
    P.add("pool", lambda e: e.affine_select(out=c.MASK, in_=c.MASK, pattern=[[1, 128]], compare_op=ALU.is_ge,
                                            fill=0.0, base=0, channel_multiplier=-1), reads=[c.constb], writes=[c.constb])
    P.add("pool", lambda e: e.memset(c.MASK[0:64, 64:128], 0.0), reads=[c.constb], writes=[c.constb])


def emit_lb(c, col0=32, col1=36, on_col=40, layer=None):
    P = c.P
    if layer == 0:
        P.add("dve", lambda e: e.memset(c.LB, 0.0), writes=[c.lbb])
    else:
        P.add("dve", lambda e: e.tensor_tensor(out=c.LB, in0=c.CON[:, col1:col1 + 4], in1=c.CON[:, col0:col0 + 4], op=ALU.subtract),
              reads=[c.conb], writes=[c.lbb])
        P.add("act", lambda e: e.activation(out=c.LB, in_=c.LB, func=AF.Sigmoid), reads=[c.lbb], writes=[c.lbb])
        if layer is None:
            P.add("dve", lambda e: e.tensor_scalar(out=c.LB, in0=c.LB, scalar1=c.CON[:, on_col:on_col + 1], scalar2=None, op0=ALU.mult),
                  reads=[c.lbb, c.conb], writes=[c.lbb])
    P.add("dve", lambda e: e.tensor_scalar(out=c.OML, in0=c.LB, scalar1=-1.0, scalar2=1.0, op0=ALU.mult, op1=ALU.add),
          reads=[c.lbb], writes=[c.lbb])
    P.add("dve", lambda e: e.tensor_scalar(out=c.NOML, in0=c.LB, scalar1=-1.0, scalar2=None, op0=ALU.add),
          reads=[c.lbb], writes=[c.lbb])


def tl(t):
    return slice(t * TT, (t + 1) * TT)


def norm_parts(c, gcol0, sq, sqb, rs, rsb, t, dst_fn=None):
    P = c.P
    nbank = 6

    def p0():
        P.add("act", lambda e: e.activation(out=sq, in_=c.X[:, :, tl(t)], func=AF.Square),
              reads=[c.Xb[n][t] for n in range(8)], writes=[sqb])

    def p1():
        for k in range(8):
            P.add("pe", lambda e, k=k: e.matmul(c.banks[nbank][:], lhsT=c.ONES, rhs=sq[:, k, :], start=(k == 0), stop=(k == 7)),
                  reads=[sqb, c.constb], writes=[c.bankb[nbank]])
        P.add("act", lambda e: e.activation(out=rs, in_=c.banks[nbank][:], func=AF.Ln, scale=1.0 / D, bias=c.EPSC),
              reads=[c.bankb[nbank], c.constb], writes=[rsb])
        P.add("act", lambda e: e.activation(out=rs, in_=rs, func=AF.Exp, scale=-0.5), reads=[rsb], writes=[rsb])

    def stt(ks):
        for k in ks:
            P.add("dve", lambda e, k=k: e.scalar_tensor_tensor(out=c.H[:, k, tl(t)], in0=c.X[:, k, tl(t)],
                                                              scalar=c.CON[:, gcol0 + k:gcol0 + k + 1], in1=rs,
                                                              op0=ALU.mult, op1=ALU.mult),
                  reads=[c.Xb[k][t], rsb, c.conb], writes=[c.Hb[t]])

    def p2():
        if dst_fn is None:
            stt(range(0, 4))
        else:
            dst_fn(t, rs, rsb)

    def p3():
        if dst_fn is None:
            stt(range(4, 8))
    return [p0, p1, p2, p3]


def emit_norm(c, gcol0, sq, sqb, rs, rsb, dst_fn=None, tiles=None):
    for t in (range(NT) if tiles is None else tiles):
        for p in norm_parts(c, gcol0, sq, sqb, rs, rsb, t, dst_fn=dst_fn):
            p()


def wload(c, out_ap, in_ap, buf, key="w"):
    return c.P.add("pool", lambda e: e.dma_start(out=out_ap, in_=in_ap), writes=[buf], dma=key)


def emit_ffn(c, wg, wu, wd, gcol0):
    P = c.P
    a = c.arena
    s = 24576
    WG = [_bf(a[:, s + i * 2048: s + (i + 1) * 2048]).rearrange("p (k n) -> p k n", k=8) for i in range(2)]
    s += 4096
    WU = [_bf(a[:, s + i * 2048: s + (i + 1) * 2048]).rearrange("p (k n) -> p k n", k=8) for i in range(2)]
    s += 4096
    WD = [_bf(a[:, s + i * 2048: s + (i + 1) * 2048]).rearrange("p (j n) -> p j n", j=4) for i in range(2)]
    s += 4096
    AB = [_bf(a[:, s + i * 1024: s + (i + 1) * 1024]).rearrange("p (j n) -> p j n", j=4) for i in range(2)]
    s += 2048
    SG = [_bf(a[:, s + i * 256: s + (i + 1) * 256]) for i in range(2)]
    s += 512
    SQ = _bf(a[:, s:s + 2048]).rearrange("p (k n) -> p k n", k=8)
    s += 2048
    RS = a[:, s:s + 512]
    s += 512
    WGb = [Buf("WG"), Buf("WG")]
    WUb = [Buf("WU"), Buf("WU")]
    WDb = [Buf("WD"), Buf("WD")]
    ABb = [Buf("AB"), Buf("AB")]
    SGb = [Buf("SG"), Buf("SG")]
    SQb = Buf("SQ")
    RSb = Buf("RS")

    wgv = wg.rearrange("(k p) n -> p k n", p=128)
    wuv = wu.rearrange("(k p) n -> p k n", p=128)
    wdv = wd.rearrange("(j p) n -> p j n", p=128)

    def load_group(gi):
        j0, G = FFN_GROUPS[gi]
        b = gi % 2
        wload(c, WG[b][:, :, 0:G * 128], wgv[:, :, j0 * 128:(j0 + G) * 128], WGb[b], "g%d" % b)
        wload(c, WU[b][:, :, 0:G * 128], wuv[:, :, j0 * 128:(j0 + G) * 128], WUb[b], "u%d" % b)
        wload(c, WD[b][:, 0:G, :], wdv[:, j0:j0 + G, :], WDb[b], "d%d" % b)

    P.barrier()
    load_group(0)

    pending = None
    cnt = 0
    units = [(gi, t) for gi in range(len(FFN_GROUPS)) for t in range(NT)]
    for ui, (gi, t) in enumerate(units):
        j0, G = FFN_GROUPS[gi]
        b = gi % 2
        ab = ui % 2
        nparts = [None] * 4
        if gi == 0:
            if t == 0:
                emit_norm(c, gcol0, SQ, SQb, RS, RSb, tiles=[0])
            if t + 1 < NT:
                nparts = norm_parts(c, gcol0, SQ, SQb, RS, RSb, t + 1)
        for jl in range(G):
            gb = cnt % 2
            ub = 2 + cnt % 2
            sgi = cnt % 2
            cnt += 1
            for k in range(8):
                P.add("pe", lambda e, k=k, jl=jl, b=b, gb=gb, t=t: e.matmul(c.banks[gb][:], lhsT=WG[b][:, k, jl * 128:(jl + 1) * 128],
                                                                            rhs=c.H[:, k, tl(t)], start=(k == 0), stop=(k == 7)),
                      reads=[WGb[b], c.Hb[t]], writes=[c.bankb[gb]])
            for k in range(8):
                P.add("pe", lambda e, k=k, jl=jl, b=b, ub=ub, t=t: e.matmul(c.banks[ub][:], lhsT=WU[b][:, k, jl * 128:(jl + 1) * 128],
                                                                            rhs=c.H[:, k, tl(t)], start=(k == 0), stop=(k == 7)),
                      reads=[WUb[b], c.Hb[t]], writes=[c.bankb[ub]])
            P.add("act", lambda e, gb=gb, sgi=sgi: e.activation(out=SG[sgi], in_=c.banks[gb][:], func=AF.Silu),
                  reads=[c.bankb[gb]], writes=[SGb[sgi]])
            P.add("dve", lambda e, ub=ub, sgi=sgi, ab=ab, jl=jl: e.tensor_tensor(out=AB[ab][:, jl, :], in0=c.banks[ub][:], in1=SG[sgi], op=ALU.mult),
                  reads=[c.bankb[ub], SGb[sgi]], writes=[ABb[ab]])
            if jl < 4 and nparts[jl] is not None:
                nparts[jl]()
        if pending is not None:
            pending()
        if t == 0 and gi + 1 < len(FFN_GROUPS):
            load_group(gi + 1)

        def down(gi=gi, t=t, G=G, b=b, ab=ab):
            for n in range(8):
                db = 4 + n % 2
                for jl in range(G):
                    P.add("pe", lambda e, n=n, jl=jl, db=db: e.matmul(c.banks[db][:], lhsT=WD[b][:, jl, n * 128:(n + 1) * 128],
                                                                      rhs=AB[ab][:, jl, :], start=(jl == 0), stop=(jl == G - 1)),
                          reads=[WDb[b], ABb[ab]], writes=[c.bankb[db]])
                P.add("dve", lambda e, n=n, db=db: e.scalar_tensor_tensor(out=c.X[:, n, tl(t)], in0=c.banks[db][:], scalar=0.5,
                                                                          in1=c.X[:, n, tl(t)], op0=ALU.mult, op1=ALU.add),
                      reads=[c.bankb[db], c.Xb[n][t]], writes=[c.Xb[n][t]])
        pending = down
    pending()


def emit_m1(c, w_in, final_fn=None, norm_gcol=None):
    P = c.P
    a = c.arena
    s = c.S0
    winv = w_in.rearrange("(k p) n -> p k n", p=128)

    def take(n):
        nonlocal s
        r = a[:, s:s + n]
        s += n
        return r
    WH = [_bf(take(1536)).rearrange("p (k j n) -> p k j n", k=8, j=3) for i in range(2)]
    WHb = [Buf("WH"), Buf("WH")]
    QTd = [_bf(take(256)) for i in range(2)]
    KTd = [_bf(take(256)) for i in range(2)]
    QBl = [take(512) for i in range(2)]
    Vt = [_bf(take(256)).rearrange("p (b v) -> p b v", b=4) for i in range(2)]
    KBd = [_bf(take(256)).rearrange("p (b v) -> p b v", b=4) for i in range(2)]
    ub_ = [Buf("unit") for i in range(2)]
    vtb = [Buf("vt") for i in range(2)]
    kbb = [Buf("kb") for i in range(2)]
    Q = take(512)
    K = take(512)
    KBT = _bf(take(256))
    Qb_, Kb_, KBTb = Buf("Q"), Buf("K"), Buf("KBT")
    SC = [take(512) for i in range(4)]
    SCb = [Buf("sc") for i in range(4)]
    BT = [take(516) for i in range(2)]
    BTb = [Buf("bt") for i in range(2)]
    ATs = [_bf(take(64)) for i in range(4)]
    ATb = [Buf("ats") for i in range(4)]
    S = [take(128) for i in range(3)]
    if final_fn is not None:
        S += [a[:, 48128 + 600 + i * 128: 48128 + 600 + (i + 1) * 128] for i in range(2)]
    NS = len(S)
    Sb = [Buf("S") for i in range(NS)]
    DEC = [take(8) for i in range(2)]
    DECb = [Buf("dec") for i in range(2)]
    assert s <= 48128, s
    sci = [0]

    def scratch():
        i = sci[0] % 4
        sci[0] += 1
        return SC[i], SCb[i]

    def load_head(h):
        b = h % 2
        for j, off in enumerate((512, 1024, 1536)):
            wload(c, WH[b][:, :, j, :], winv[:, :, off + h * 128: off + (h + 1) * 128], WHb[b], "gud"[j] + str(b))

    P.barrier()
    load_head(0)
    BK_Q, BK_F, BK_V, BK_AT, BK_O, BK_U, BK_T = 0, 1, 2, 3, 4, 5, 7
    pst = c.banks[BK_T][:].bitcast(BF16)
    atc = [0]
    sidx = [0]

    def v3(x):
        return x.rearrange("p (c j) -> p c j", j=64)

    if norm_gcol is not None:
        NSQ = _bf(a[:, 24576 + 3 * 2048: 24576 + 4 * 2048]).rearrange("p (k n) -> p k n", k=8)
        NRS = a[:, 24576 + 2 * 2048 + 1536: 24576 + 3 * 2048]
        NSQb, NRSb = Buf("nsq"), Buf("nrs")
    else:
        NSQb = NRSb = None

    def front_segs(ui):
        h, t = divmod(ui, NT)
        u = ui % 2
        wb = h % 2
        bt = BT[t % 2]
        btb = BTb[t % 2]
        btm = bt[:, 1:513].rearrange("p (c j) -> p c j", j=64)
        bts = bt[:, 0:512].rearrange("p (c j) -> p c j", j=64)
        rmid = btm[:, :, 31:32].broadcast_to([128, 8, 64])
        bst = bts[:, :, 0:1].broadcast_to([128, 8, 64])
        ben = btm[:, :, 63:64].broadcast_to([128, 8, 64])
        st = {}

        def seg0():
            if norm_gcol is not None and h == 0:
                emit_norm(c, norm_gcol, NSQ, NSQb, NRS, NRSb, tiles=[t])
            if t == 0:
                P.add("dve", lambda e: e.memset(BT[0][:, 0:1], 0.0), writes=[BTb[0]])
            for k in range(8):
                P.add("pe", lambda e, k=k: e.matmul(c.banks[BK_Q][:], lhsT=WH[wb][:, k, 0, :], rhs=c.H[:, k, tl(t)], start=(k == 0), stop=(k == 7)),
                      reads=[WHb[wb], c.Hb[t]], writes=[c.bankb[BK_Q]])
            for k in range(8):
                P.add("pe", lambda e, k=k: e.matmul(c.banks[BK_F][:], lhsT=WH[wb][:, k, 1, :], rhs=c.H[:, k, tl(t)], start=(k == 0), stop=(k == 7)),
                      reads=[WHb[wb], c.Hb[t]], writes=[c.bankb[BK_F]])
            for blk in range(4):
                for k in range(8):
                    P.add("pe", lambda e, k=k, blk=blk: e.matmul(c.banks[BK_V][:, blk * 128:(blk + 1) * 128],
                                                                 lhsT=c.H[:, k, t * TT + blk * 128: t * TT + (blk + 1) * 128],
                                                                 rhs=WH[wb][:, k, 2, :], start=(k == 0), stop=(k == 7)),
                          reads=[WHb[wb], c.Hb[t]], writes=[c.bankb[BK_V]])
            if t == 0 and h + 1 < 4:
                load_head(h + 1)
            sq_, sqb_ = scratch()
            P.add("act", lambda e: e.activation(out=sq_, in_=c.banks[BK_Q][:], func=AF.Sigmoid), reads=[c.bankb[BK_Q]], writes=[sqb_])
            sig, sigb = scratch()
            st["sig"], st["sigb"] = sig, sigb
            P.add("act", lambda e: e.activation(out=sig, in_=c.banks[BK_F][:], func=AF.Sigmoid), reads=[c.bankb[BK_F]], writes=[sigb])
            P.add("act", lambda e: e.activation(out=Vt[u], in_=c.banks[BK_V][:].rearrange("p (b v) -> p b v", b=4), func=AF.Copy),
                  reads=[c.bankb[BK_V]], writes=[vtb[u]])
            P.add("dve", lambda e: e.tensor_tensor(out=Q, in0=c.banks[BK_Q][:], in1=sq_, op=ALU.mult), reads=[c.bankb[BK_Q], sqb_], writes=[Qb_])
            P.add("dve", lambda e: e.tensor_scalar(out=K, in0=sig, scalar1=c.NOML[:, h:h + 1], scalar2=c.OML[:, h:h + 1], op0=ALU.mult, op1=ALU.add),
                  reads=[sigb, c.lbb], writes=[Kb_])

        def seg1():
            sig, sigb = st["sig"], st["sigb"]
            lf, lfb = scratch()
            P.add("act", lambda e: e.activation(out=lf, in_=sig, func=AF.Ln, scale=c.OML[:, h:h + 1], bias=c.LB[:, h:h + 1]),
                  reads=[sigb, c.lbb], writes=[lfb])
            P.add("dve", lambda e: e.tensor_tensor_scan(bt[:, 1:513], c.ONEC.broadcast_to([128, 512]), lf, bt[:, 0:1], ALU.mult, ALU.add),
                  reads=[lfb, btb, c.constb], writes=[btb])
            if t + 1 < NT:
                nb = BT[(t + 1) % 2]
                P.add("dve", lambda e: e.tensor_copy(out=nb[:, 0:1], in_=bt[:, 512:513]), reads=[btb], writes=[BTb[(t + 1) % 2]])
            d1, d1b = scratch()
            P.add("dve", lambda e: e.tensor_tensor(out=v3(d1), in0=btm, in1=rmid, op=ALU.subtract), reads=[btb], writes=[d1b])
            e1, e1b = scratch()
            P.add("act", lambda e: e.activation(out=e1, in_=d1, func=AF.Exp), reads=[d1b], writes=[e1b])
            P.add("dve", lambda e: e.tensor_tensor(out=QTd[u], in0=Q, in1=e1, op=ALU.mult), reads=[Qb_, e1b], writes=[ub_[u]])
            e2, e2b = scratch()
            P.add("act", lambda e: e.activation(out=e2, in_=d1, func=AF.Exp, scale=-1.0), reads=[d1b], writes=[e2b])
            P.add("pool", lambda e: e.tensor_tensor(out=KTd[u], in0=K, in1=e2, op=ALU.mult), reads=[Kb_, e2b], writes=[ub_[u]])

        def seg2():
            d2, d2b = scratch()
            P.add("dve", lambda e: e.tensor_tensor(out=v3(d2), in0=btm, in1=bst, op=ALU.subtract), reads=[btb], writes=[d2b])
            e3, e3b = scratch()
            P.add("act", lambda e: e.activation(out=e3, in_=d2, func=AF.Exp), reads=[d2b], writes=[e3b])
            P.add("pool", lambda e: e.tensor_tensor(out=QBl[u], in0=Q, in1=e3, op=ALU.mult), reads=[Qb_, e3b], writes=[ub_[u]])
            d3, d3b = scratch()
            P.add("dve", lambda e: e.tensor_tensor(out=v3(d3), in0=btm, in1=ben, op=ALU.subtract), reads=[btb], writes=[d3b])
            e4, e4b = scratch()
            P.add("act", lambda e: e.activation(out=e4, in_=d3, func=AF.Exp, scale=-1.0), reads=[d3b], writes=[e4b])
            P.add("pool", lambda e: e.tensor_tensor(out=KBT, in0=K, in1=e4, op=ALU.mult), reads=[Kb_, e4b], writes=[KBTb])

        def seg3():
            e5, e5b = scratch()
            P.add("act", lambda e: e.activation(out=e5, in_=bt[:, 1:513], func=AF.Exp), reads=[btb], writes=[e5b])
            P.add("pool", lambda e: e.tensor_tensor(out=c.QB[:, h, tl(t)], in0=Q, in1=e5, op=ALU.mult), reads=[Qb_, e5b], writes=[c.QBb[h][t]])
            dec = DEC[u]
            P.add("dve", lambda e: e.tensor_tensor(out=dec, in0=bt[:, 64:513:64], in1=bt[:, 0:512:64], op=ALU.subtract), reads=[btb], writes=[DECb[u]])
            P.add("act", lambda e: e.activation(out=dec, in_=dec, func=AF.Exp), reads=[DECb[u]], writes=[DECb[u]])
            if t == NT - 1:
                P.add("act", lambda e: e.activation(out=c.DOUT[:, h:h + 1], in_=bt[:, 512:513], func=AF.Exp), reads=[btb], writes=[c.uoutb])
            for blk in range(4):
                P.add("pe", lambda e, blk=blk: e.transpose(pst[:, blk * 128:(blk + 1) * 128], KBT[:, blk * 128:(blk + 1) * 128], c.IDENT),
                      reads=[KBTb, c.constb], writes=[c.bankb[BK_T]])
            P.add("act", lambda e: e.activation(out=KBd[u], in_=pst[:, 0:512].rearrange("p (b v) -> p b v", b=4), func=AF.Copy),
                  reads=[c.bankb[BK_T]], writes=[kbb[u]])
        return [seg0, seg1, seg2, seg3]

    def back_blocks(ui):
        h, t = divmod(ui, NT)
        u = ui % 2
        info = {}

        def mk1(blk):
            def f():
                if t == 0 and blk == 0:
                    sidx[0] = 0
                    P.add("dve", lambda e: e.memset(S[0], 0.0), writes=[Sb[0]])
                ai = atc[0] % 4
                atc[0] += 1
                tok = slice(blk * 128, (blk + 1) * 128)
                atp = c.banks[BK_AT][:, ai * 128:(ai + 1) * 128]
                P.add("pe", lambda e: e.matmul(atp, lhsT=KTd[u][:, tok], rhs=QTd[u][:, tok], start=True, stop=True),
                      reads=[ub_[u]], writes=[c.bankb[BK_AT]])
                P.add("dve", lambda e: e.tensor_tensor(out=ATs[ai], in0=atp, in1=c.MASK, op=ALU.mult),
                      reads=[c.bankb[BK_AT], c.constb], writes=[ATb[ai]])
                s0, s1, s2 = sidx[0] % NS, (sidx[0] + 1) % NS, (sidx[0] + 2) % NS
                sidx[0] += 2
                ubk = [BK_U, 6]
                up = [c.banks[ubk[i]][:, (ai % 4) * 128:(ai % 4 + 1) * 128] for i in range(2)]
                for i in range(2):
                    pr = slice(i * 64, (i + 1) * 64)
                    P.add("pe", lambda e, pr=pr, i=i: e.matmul(up[i], lhsT=KBd[u][pr, blk, :], rhs=Vt[u][pr, blk, :], start=True, stop=True),
                          reads=[kbb[u], vtb[u]], writes=[c.bankb[ubk[i]]])
                cA = blk * 2
                P.add("dve", lambda e: e.scalar_tensor_tensor(out=S[s1], in0=S[s0], scalar=DEC[u][:, cA:cA + 1], in1=up[0], op0=ALU.mult, op1=ALU.add),
                      reads=[Sb[s0], DECb[u], c.bankb[BK_U]], writes=[Sb[s1]])
                P.add("dve", lambda e: e.scalar_tensor_tensor(out=S[s2], in0=S[s1], scalar=DEC[u][:, cA + 1:cA + 2], in1=up[1], op0=ALU.mult, op1=ALU.add),
                      reads=[Sb[s1], DECb[u], c.bankb[6]], writes=[Sb[s2]])
                info[blk] = (ai, s0, s1)
                if t == NT - 1 and blk == 3:
                    sl = sidx[0] % NS
                    if final_fn is None:
                        P.add("dve", lambda e: e.tensor_copy(out=c.UOUT[:, h, :], in_=S[sl]), reads=[Sb[sl]], writes=[c.uoutb])
                    else:
                        final_fn(h, S[sl], Sb[sl])
            return f

        def mk2(blk):
            def f():
                ai, s0, s1 = info[blk]
                op_ = c.banks[BK_O][:, ai * 128:(ai + 1) * 128]
                P.add("pe", lambda e: e.matmul(op_, lhsT=Vt[u][:, blk, :], rhs=ATs[ai], start=True, stop=False),
                      reads=[vtb[u], ATb[ai]], writes=[c.bankb[BK_O]])
                P.add("pe", lambda e: e.matmul(op_[:, 0:64], lhsT=S[s0], rhs=QBl[u][:, blk * 128: blk * 128 + 64], start=False, stop=False),
                      reads=[Sb[s0], ub_[u]], writes=[c.bankb[BK_O]])
                P.add("pe", lambda e: e.matmul(op_[:, 64:128], lhsT=S[s1], rhs=QBl[u][:, blk * 128 + 64: blk * 128 + 128], start=False, stop=True),
                      reads=[Sb[s1], ub_[u]], writes=[c.bankb[BK_O]])
                P.add("act", lambda e: e.activation(out=c.OL[:, h, t * TT + blk * 128: t * TT + (blk + 1) * 128], in_=op_, func=AF.Copy),
                      reads=[c.bankb[BK_O]], writes=[c.OLb[h][t]] + ([NSQb, NRSb] if (NSQb is not None and h >= 2) else []))
            return f
        return [mk1(b) for b in range(4)], [mk2(b) for b in range(4)]

    NU = 4 * NT
    for f in front_segs(0):
        f()
    for ui in range(NU):
        p1, p2 = back_blocks(ui)
        ff = front_segs(ui + 1) if ui + 1 < NU else [None] * 4
        p1[0]()
        for i in range(4):
            if ff[i] is not None:
                ff[i]()
            if i + 1 < 4:
                p1[i + 1]()
            p2[i]()


def emit_m2(c, w_in, pool_w, w_pa, w_pb, w_out, uall_d, dall_d, halo_d, ps_col=24, hn_col=28, cc_out=None, ccoutb=None):
    P = c.P
    a = c.arena
    winv = w_in.rearrange("(k p) n -> p k n", p=128)
    s = c.S0

    def take(n):
        nonlocal s
        r = a[:, s:s + n]
        s += n
        return r
    PO_BASE = 48128 - 4096
    s = c.S0
    PO = _bf(a[:, PO_BASE:PO_BASE + 4096]).rearrange("p (g t) -> p g t", g=4)
    POb = [[Buf() for t in range(NT)] for g in range(4)]
    olall = []
    WP = [_bf(take(512)).rearrange("p (k n) -> p k n", k=8) for i in range(4)]
    WPb = [Buf() for i in range(4)]
    PW = [_bf(take(64)) for i in range(4)]
    PWb = [Buf() for i in range(4)]
    HALO = _bf(take(64)).rearrange("p (k j) -> p k j", k=8)
    halob = Buf()
    UB = [take(528) for i in range(2)]
    UBb = [Buf() for i in range(2)]
    PP = [take(528) for i in range(2)]
    PPb = [Buf() for i in range(2)]
    DIFF = [_bf(take(256)) for i in range(2)]
    dfb = [Buf() for i in range(2)]
    P.barrier(skip=("dma_cc",))
    HT = take(192)
    htb = Buf()

    def load_halo():
        ccv = cc_out.rearrange("(r p) n -> p r n", p=128)
        P.add("sp", lambda e: e.dma_start(out=HT.rearrange("p (r n) -> p r n", r=3), in_=ccv[:, 0:3, 516:580]), reads=[ccoutb], writes=[htb], dma="x")
        hflat = HALO.rearrange("p k j -> p (k j)")
        htv = _bf(HT).rearrange("p (r n) -> p r n", r=3)
        P.add("dve", lambda e: e.tensor_scalar(out=hflat, in0=htv[:, 0, :], scalar1=c.COEF[:, 4:5], scalar2=None, op0=ALU.mult),
              reads=[htb, c.coefb], writes=[halob])
        for r_ in (1, 2):
            P.add("dve", lambda e, r_=r_: e.scalar_tensor_tensor(out=hflat, in0=htv[:, r_, :], scalar=c.COEF[:, 4 + r_:5 + r_], in1=hflat, op0=ALU.mult, op1=ALU.add),
                  reads=[htb, c.coefb, halob], writes=[halob])

    def load_pool(g):
        wload(c, WP[g], winv[:, :, g * 128:(g + 1) * 128], WPb[g], "g%d" % (g % 2))
        wload(c, PW[g], pool_w[g], PWb[g], "u%d" % (g % 2))
    for g_ in range(4):
        load_pool(g_)
    B_U, B_UH, B_MX = 0, 6, 2
    PP2 = [take(528) for i in range(2)]
    PP2b = [Buf() for i in range(2)]
    assert s <= PO_BASE, s
    bunits = [(g, t) for g in range(4) for t in range(1, NT)] + [(g, 0) for g in range(4)]

    def bA(ui):
        g, t = bunits[ui]
        u = ui % 2
        wb = g
        bu = B_U + u
        for k in range(8):
            P.add("pe", lambda e, k=k: e.matmul(c.banks[bu][:], lhsT=WP[wb][:, k, :], rhs=c.H[:, k, tl(t)], start=(k == 0), stop=(k == 7)),
                  reads=[WPb[wb], c.Hb[t]], writes=[c.bankb[bu]])
        if t == 0:
            for k in range(8):
                P.add("pe", lambda e, k=k: e.matmul(c.banks[B_UH][:, 0:16], lhsT=WP[wb][:, k, :], rhs=HALO[:, k, :], start=(k == 0), stop=(k == 7)),
                      reads=[WPb[wb], halob], writes=[c.bankb[B_UH]])
            P.add("dve", lambda e: e.tensor_scalar(out=UB[u][:, 0:16], in0=c.banks[B_UH][:, 0:16], scalar1=c.COEF[:, 3:4], scalar2=None, op0=ALU.mult),
                  reads=[c.bankb[B_UH], c.coefb], writes=[UBb[u]])
        else:
            for k in range(8):
                P.add("pe", lambda e, k=k: e.matmul(c.banks[B_UH][:, 0:16], lhsT=WP[wb][:, k, :], rhs=c.H[:, k, t * TT - 16:t * TT], start=(k == 0), stop=(k == 7)),
                      reads=[WPb[wb], c.Hb[t - 1]], writes=[c.bankb[B_UH]])
            P.add("act", lambda e: e.activation(out=UB[u][:, 0:16], in_=c.banks[B_UH][:, 0:16], func=AF.Copy), reads=[c.bankb[B_UH]], writes=[UBb[u]])
        P.add("act", lambda e: e.activation(out=UB[u][:, 16:528], in_=c.banks[bu][:], func=AF.Copy), reads=[c.bankb[bu]], writes=[UBb[u]])

    def bB(ui):
        g, t = bunits[ui]
        u = ui % 2
        wb = g
        w = 2 ** (g + 1)
        bmx = B_MX + u
        eng = "dve" if ui % 2 == 0 else "pool"
        pp, ppb = (PP, PPb) if eng == "dve" else (PP2, PP2b)
        src, srcb = UB[u], UBb[u]
        step, lo, pi = 1, 0, 0
        while step < w:
            dst, dstb = pp[pi % 2], ppb[pi % 2]
            pi += 1
            nlo = lo + step
            P.add(eng, lambda e, src=src, dst=dst, nlo=nlo, step=step: e.tensor_tensor(out=dst[:, nlo:528], in0=src[:, nlo:528], in1=src[:, nlo - step:528 - step], op=ALU.add),
                  reads=[srcb], writes=[dstb])
            src, srcb = dst, dstb
            lo = nlo
            step *= 2
        plf, plfb = src[:, 16:528], srcb
        P.add(eng, lambda e: e.tensor_scalar(out=plf, in0=plf, scalar1=1.0 / w, scalar2=0.0, op0=ALU.mult, op1=ALU.add), reads=[srcb], writes=[srcb])
        if t == 0:
            P.add(eng, lambda e: e.tensor_tensor(out=plf[:, 0:16], in0=plf[:, 0:16], in1=c.FIX[:, g * 16:(g + 1) * 16], op=ALU.mult),
                  reads=[plfb, c.coefb], writes=[plfb])
        P.add(eng, lambda e: e.tensor_tensor(out=DIFF[u], in0=plf, in1=UB[u][:, 16:528], op=ALU.subtract), reads=[plfb, UBb[u]], writes=[dfb[u]])
        P.add("pe", lambda e: e.matmul(c.banks[bmx][:], lhsT=PW[wb], rhs=DIFF[u], start=True, stop=True),
              reads=[PWb[wb], dfb[u]], writes=[c.bankb[bmx]])
        P.add("dve", lambda e: e.tensor_scalar(out=PO[:, g, tl(t)], in0=c.banks[bmx][:], scalar1=c.CON[:, ps_col + g:ps_col + 1 + g], scalar2=None, op0=ALU.mult),
              reads=[c.bankb[bmx], c.conb] + olall, writes=[POb[g][t]])

    bA(0)
    for ui in range(len(bunits)):
        if ui + 1 < len(bunits):
            if bunits[ui + 1] == (0, 0):
                load_halo()
            bA(ui + 1)
        bB(ui)
    s = c.S0
    UALL = take(1536).rearrange("p (i h v) -> p i h v", i=3, h=4)
    DALL = take(12).rearrange("p (i h) -> p i h", i=3)
    ualb = Buf("uall")
    SINf = take(512).rearrange("p (h v) -> p h v", h=4)
    SINb = _bf(take(256)).rearrange("p (h v) -> p h v", h=4)
    sinb = Buf("sin")
    WOG = [_bf(take(512)).rearrange("p (k n) -> p k n", k=8) for i in range(2)]
    WOGb = [Buf("wog") for i in range(2)]
    OS = [take(512) for i in range(2)]
    SQh = [_bf(take(256)) for i in range(2)]
    RSh = [take(512) for i in range(2)]
    SOG = [take(512) for i in range(2)]
    T1 = OS
    osb = [Buf() for i in range(2)]
    sqb = [Buf() for i in range(2)]
    rsb = [Buf() for i in range(2)]
    sogb = [Buf() for i in range(2)]
    t1b = osb
    assert s <= PO_BASE
    OGc = [a[:, c.S0 + i * 512: c.S0 + (i + 1) * 512] for i in range(2)]
    ogcb = [Buf() for i in range(2)]
    P.barrier()
    if cc_out is None:
        P.add("sp", lambda e: e.dma_start(out=UALL.rearrange("p i h v -> p (i h v)"), in_=uall_d), writes=[ualb], dma="x")
        P.add("sp", lambda e: e.dma_start(out=DALL.rearrange("p i h -> p (i h)"), in_=dall_d), writes=[ualb], dma="x")
    else:
        ccv = cc_out.rearrange("(r p) n -> p r n", p=128)
        P.add("sp", lambda e: e.dma_start(out=UALL.rearrange("p i h v -> p i (h v)"), in_=ccv[:, 0:3, 0:512]), reads=[ccoutb], writes=[ualb], dma="x")
        P.add("sp", lambda e: e.dma_start(out=DALL, in_=ccv[:, 0:3, 512:516]), reads=[ccoutb], writes=[ualb], dma="x")

    def load_og(h):
        wload(c, WOG[h % 2], winv[:, :, 2048 + h * 128: 2048 + (h + 1) * 128], WOGb[h % 2], "g%d" % (h % 2))
    load_og(0)
    for i in (1, 2):
        for h in range(4):
            P.add("dve", lambda e, i=i, h=h: e.scalar_tensor_tensor(out=UALL[:, i, h, :], in0=UALL[:, i - 1, h, :], scalar=DALL[:, i, h:h + 1],
                                                                  in1=UALL[:, i, h, :], op0=ALU.mult, op1=ALU.add),
                  reads=[ualb], writes=[ualb])
    sf = SINf.rearrange("p h v -> p (h v)")
    P.add("dve", lambda e: e.tensor_scalar(out=sf, in0=UALL[:, 0].rearrange("p h v -> p (h v)"), scalar1=c.COEF[:, 0:1], scalar2=None, op0=ALU.mult),
          reads=[ualb, c.coefb], writes=[sinb])
    for i in (1, 2):
        P.add("dve", lambda e, i=i: e.scalar_tensor_tensor(out=sf, in0=UALL[:, i].rearrange("p h v -> p (h v)"), scalar=c.COEF[:, i:i + 1], in1=sf,
                                                          op0=ALU.mult, op1=ALU.add),
              reads=[ualb, c.coefb, sinb], writes=[sinb])
    P.add("dve", lambda e: e.tensor_copy(out=SINb.rearrange("p h v -> p (h v)"), in_=sf), reads=[sinb], writes=[sinb])
    B_OC, B_OG, B_MS = 0, 1, 2
    units = [(h, t) for h in range(4) for t in range(NT)]

    def stageA(ui):
        h, t = units[ui]
        u = ui % 2
        wb = h % 2
        boc, bog, bms = B_OC + 3 * u, B_OG + 3 * u, B_MS + 3 * u
        if t == 0 and h + 1 < 4:
            load_og(h + 1)
        P.add("pe", lambda e: e.matmul(c.banks[boc][:], lhsT=SINb[:, h, :], rhs=c.QB[:, h, tl(t)], start=True, stop=True),
              reads=[sinb, c.QBb[h][t]], writes=[c.bankb[boc]])
        for k in range(8):
            P.add("pe", lambda e, k=k: e.matmul(c.banks[bog][:], lhsT=WOG[wb][:, k, :], rhs=c.H[:, k, tl(t)], start=(k == 0), stop=(k == 7)),
                  reads=[WOGb[wb], c.Hb[t]], writes=[c.bankb[bog]])
        P.add("dve", lambda e: e.tensor_tensor(out=OS[u], in0=c.banks[boc][:], in1=c.OL[:, h, tl(t)], op=ALU.add),
              reads=[c.bankb[boc], c.OLb[h][t]], writes=[osb[u]])
        P.add("act", lambda e: e.activation(out=SQh[u], in_=OS[u], func=AF.Square), reads=[osb[u]], writes=[sqb[u]])
        P.add("pe", lambda e: e.matmul(c.banks[bms][:], lhsT=c.ONES, rhs=SQh[u], start=True, stop=True),
              reads=[sqb[u], c.constb], writes=[c.bankb[bms]])
        P.add("act", lambda e: e.activation(out=SOG[u], in_=c.banks[bog][:], func=AF.Exp, scale=-1.0), reads=[c.bankb[bog]], writes=[sogb[u]])
        P.add("act", lambda e: e.activation(out=OGc[u], in_=c.banks[bog][:], func=AF.Copy), reads=[c.bankb[bog]], writes=[ogcb[u], ualb])

    def stageB(ui):
        h, t = units[ui]
        u = ui % 2
        boc, bog, bms = B_OC + 3 * u, B_OG + 3 * u, B_MS + 3 * u
        P.add("act", lambda e: e.activation(out=RSh[u], in_=c.banks[bms][:], func=AF.Ln, scale=1.0 / 128, bias=c.EPSC),
              reads=[c.bankb[bms], c.constb], writes=[rsb[u]])
        P.add("act", lambda e: e.activation(out=RSh[u], in_=RSh[u], func=AF.Exp, scale=-0.5), reads=[rsb[u]], writes=[rsb[u]])
        P.add("act", lambda e: e.activation(out=SOG[u], in_=SOG[u], func=AF.Ln, scale=1.0, bias=c.ONEC), reads=[sogb[u], c.constb], writes=[sogb[u]])
        P.add("act", lambda e: e.activation(out=SOG[u], in_=SOG[u], func=AF.Exp, scale=-1.0), reads=[sogb[u]], writes=[sogb[u]])
        P.add("dve", lambda e: e.tensor_tensor(out=SOG[u], in0=OGc[u], in1=SOG[u], op=ALU.mult), reads=[ogcb[u], sogb[u]], writes=[sogb[u]])
        P.add("dve", lambda e: e.scalar_tensor_tensor(out=T1[u], in0=OS[u], scalar=c.CON[:, hn_col + h:hn_col + 1 + h], in1=RSh[u], op0=ALU.mult, op1=ALU.mult),
              reads=[osb[u], rsb[u], c.conb], writes=[t1b[u], osb[u]])
        P.add("pool", lambda e: e.tensor_tensor(out=c.QB[:, h, tl(t)], in0=T1[u], in1=SOG[u], op=ALU.mult),
              reads=[t1b[u], sogb[u]], writes=[c.QBb[h][t]])

    stageA(0)
    for ui in range(len(units)):
        if ui + 1 < len(units):
            stageA(ui + 1)
        stageB(ui)
    HGO, HGOb = c.QB, c.QBb
    s = c.S0
    Mall = _bf(a[:, 24576:32768]).rearrange("p (n t) -> p n t", n=8)
    olall = [c.OLb[h][t] for h in range(4) for t in range(NT)]

    def Mv(n, t):
        return Mall[:, n, tl(t)]
    Mb = [[Buf() for t in range(NT)] for n in range(8)]
    WGA = [_bf(take(512)).rearrange("p (k n) -> p k n", k=8) for i in range(2)]
    WGB = [_bf(take(512)).rearrange("p (k n) -> p k n", k=8) for i in range(2)]
    WPA = [_bf(take(256)).rearrange("p (g n) -> p g n", g=4) for i in range(2)]
    WPB = [_bf(take(256)).rearrange("p (g n) -> p g n", g=4) for i in range(2)]
    wcb = [Buf() for i in range(2)]
    SGA = [_bf(take(256)) for i in range(2)]
    SGB = [_bf(take(256)) for i in range(2)]
    TA = [take(512) for i in range(2)]
    TB = [take(512) for i in range(2)]
    sgab = [Buf() for i in range(2)]
    sgbb = [Buf() for i in range(2)]
    tab = [Buf() for i in range(2)]
    tbb = [Buf() for i in range(2)]
    assert s <= PO_BASE
    wpav = w_pa.rearrange("(g p) n -> p g n", p=128)
    wpbv = w_pb.rearrange("(g p) n -> p g n", p=128)

    def load_c(n):
        b = n % 2
        wload(c, WGA[b], winv[:, :, 2560 + n * 128: 2560 + (n + 1) * 128], wcb[b], "g%d" % b)
        wload(c, WGB[b], winv[:, :, 3584 + n * 128: 3584 + (n + 1) * 128], wcb[b], "u%d" % b)
        wload(c, WPA[b], wpav[:, :, n * 128:(n + 1) * 128], wcb[b], "d%d" % b)
        wload(c, WPB[b], wpbv[:, :, n * 128:(n + 1) * 128], wcb[b], "e%d" % b)
    P.barrier()
    load_c(0)
    ui = 0
    for n in range(8):
        if n + 1 < 8:
            load_c(n + 1)
        wb = n % 2
        for t in range(NT):
            u = ui % 2
            ui += 1
            bga, bgb, bpa, bpb = 0 + 4 * u, 1 + 4 * u, 2 + 4 * u, 3 + 4 * u
            for k in range(8):
                P.add("pe", lambda e, k=k, t=t, bga=bga: e.matmul(c.banks[bga][:], lhsT=WGA[wb][:, k, :], rhs=c.H[:, k, tl(t)], start=(k == 0), stop=(k == 7)),
                      reads=[wcb[wb], c.Hb[t]], writes=[c.bankb[bga]])
            for k in range(8):
                P.add("pe", lambda e, k=k, t=t, bgb=bgb: e.matmul(c.banks[bgb][:], lhsT=WGB[wb][:, k, :], rhs=c.H[:, k, tl(t)], start=(k == 0), stop=(k == 7)),
                      reads=[wcb[wb], c.Hb[t]], writes=[c.bankb[bgb]])
            for g in range(4):
                P.add("pe", lambda e, g=g, t=t, bpa=bpa: e.matmul(c.banks[bpa][:], lhsT=WPA[wb][:, g, :], rhs=PO[:, g, tl(t)], start=(g == 0), stop=(g == 3)),
                      reads=[wcb[wb], POb[g][t]], writes=[c.bankb[bpa]])
            for g in range(4):
                P.add("pe", lambda e, g=g, t=t, bpb=bpb: e.matmul(c.banks[bpb][:], lhsT=WPB[wb][:, g, :], rhs=HGO[:, g, tl(t)], start=(g == 0), stop=(g == 3)),
                      reads=[wcb[wb], HGOb[g][t]], writes=[c.bankb[bpb]])
            P.add("act", lambda e, u=u, bga=bga: e.activation(out=SGA[u], in_=c.banks[bga][:], func=AF.Sigmoid), reads=[c.bankb[bga]], writes=[sgab[u]])
            P.add("act", lambda e, u=u, bgb=bgb: e.activation(out=SGB[u], in_=c.banks[bgb][:], func=AF.Sigmoid), reads=[c.bankb[bgb]], writes=[sgbb[u]])
            P.add("dve", lambda e, u=u, bpa=bpa: e.tensor_tensor(out=TA[u], in0=c.banks[bpa][:], in1=SGA[u], op=ALU.mult), reads=[c.bankb[bpa], sgab[u]], writes=[tab[u]])
            P.add("dve", lambda e, u=u, bpb=bpb: e.tensor_tensor(out=TB[u], in0=c.banks[bpb][:], in1=SGB[u], op=ALU.mult), reads=[c.bankb[bpb], sgbb[u]], writes=[tbb[u]])
            P.add("pool", lambda e, u=u, n=n, t=t: e.tensor_tensor(out=Mv(n, t), in0=TA[u], in1=TB[u], op=ALU.add),
                  reads=[tab[u], tbb[u]] + olall, writes=[Mb[n][t]])
    WO = [_bf(take(512)).rearrange("p (k n) -> p k n", k=8) for i in range(2)]
    WOb = [Buf() for i in range(2)]
    assert s <= PO_BASE
    wov = w_out.rearrange("(k p) n -> p k n", p=128)

    def load_o(n):
        wload(c, WO[n % 2], wov[:, :, n * 128:(n + 1) * 128], WOb[n % 2], "g%d" % (n % 2))
    load_o(0)
    ui = 0
    for n in range(8):
        if n + 1 < 8:
            load_o(n + 1)
        wb = n % 2
        for t in range(NT):
            bo = ui % 2
            ui += 1
            for k in range(8):
                P.add("pe", lambda e, k=k, t=t, bo=bo: e.matmul(c.banks[bo][:], lhsT=WO[wb][:, k, :], rhs=Mv(k, t), start=(k == 0), stop=(k == 7)),
                      reads=[WOb[wb], Mb[k][t]], writes=[c.bankb[bo]])
            P.add("dve", lambda e, n=n, t=t, bo=bo: e.tensor_tensor(out=c.X[:, n, tl(t)], in0=c.banks[bo][:], in1=c.X[:, n, tl(t)], op=ALU.add),
                  reads=[c.bankb[bo], c.Xb[n][t]], writes=[c.Xb[n][t]])


def emit_final_norm(c, out_d, gcol0=16):
    P = c.P
    a = c.arena
    s = c.S0
    SQ = _bf(a[:, s:s + 2048]).rearrange("p (k n) -> p k n", k=8)
    RS = a[:, s + 2048:s + 2560]
    OB = [a[:, s + 2560 + i * 4096: s + 2560 + (i + 1) * 4096].rearrange("p (k n) -> p k n", k=8) for i in range(2)]
    SQb, RSb = Buf(), Buf()
    OBb = [Buf(), Buf()]
    outv = out_d.rearrange("(k p) t -> p k t", p=128)
    stores = []

    def dst(t, rs, rsb):
        ob = OB[t % 2]
        for k in range(8):
            P.add("dve", lambda e, k=k, t=t, ob=ob: e.scalar_tensor_tensor(out=ob[:, k, :], in0=c.X[:, k, tl(t)], scalar=c.CON[:, gcol0 + k:gcol0 + 1 + k], in1=rs,
                                                                         op0=ALU.mult, op1=ALU.mult),
                  reads=[c.Xb[k][t], rsb, c.conb], writes=[OBb[t % 2]])
        stores.append(P.add("sp", lambda e, t=t, ob=ob: e.dma_start(out=outv[:, :, tl(t)], in_=ob), reads=[OBb[t % 2]], dma="o"))
    P.barrier()
    emit_norm(c, gcol0, SQ, SQb, RS, RSb, dst_fn=dst)
    return stores


def load_small(c, con_d, coef_d, fix_d):
    P = c.P
    P.add("sp", lambda e: e.dma_start(out=c.CON, in_=con_d), writes=[c.conb], dma="x")
    P.add("sp", lambda e: e.dma_start(out=c.COEF, in_=coef_d), writes=[c.coefb], dma="x")
    P.add("sp", lambda e: e.dma_start(out=c.FIX, in_=fix_d), writes=[c.coefb], dma="x")


def all_state_bufs(c):
    r = [c.Xb[n][t] for n in range(8) for t in range(NT)] + list(c.Hb)
    r += [c.OLb[h][t] for h in range(4) for t in range(NT)] + [c.QBb[h][t] for h in range(4) for t in range(NT)]
    return r


def build_A():
    nc = bass.Bass("TRN2", target_bir_lowering=False)
    xT = nc.dram_tensor("xT", [D, T], F32, kind="ExternalInput").ap()
    con = nc.dram_tensor("con", [128, 64], F32, kind="ExternalInput").ap()
    coef = nc.dram_tensor("coef", [128, 8], F32, kind="ExternalInput").ap()
    fix = nc.dram_tensor("fix", [128, 64], F32, kind="ExternalInput").ap()
    wg = nc.dram_tensor("wg", [D, DFF], F32, kind="ExternalInput").ap()
    wu = nc.dram_tensor("wu", [D, DFF], F32, kind="ExternalInput").ap()
    wd = nc.dram_tensor("wd", [DFF, D], F32, kind="ExternalInput").ap()
    w_in = nc.dram_tensor("w_in", [D, DIN], F32, kind="ExternalInput").ap()
    state = nc.dram_tensor("state", [128, STATE_WORDS], F32, kind="ExternalOutput").ap()
    uout = nc.dram_tensor("uout", [128, 512], F32, kind="ExternalOutput").ap()
    dout = nc.dram_tensor("dout", [128, 4], F32, kind="ExternalOutput").ap()
    with contextlib.ExitStack() as st:
        P = Prog(nc)
        c = setup_common(nc, P, st)
        load_small(c, con, coef, fix)
        xv = xT.rearrange("(k p) t -> p k t", p=128)
        for k in range(8):
            P.add("sp", lambda e, k=k: e.dma_start(out=c.X[:, k, :], in_=xv[:, k, :]), writes=[c.Xb[k][t] for t in range(NT)], dma="x")
        emit_consts(c)
        if "lb" in STAGES:
            emit_lb(c)
        if "ffn" in STAGES:
            emit_ffn(c, wg, wu, wd, 0)
        if "normmix" in STAGES:
            emit_norm_mix(c)
        if "m1" in STAGES:
            emit_m1(c, w_in)
        outs = []
        sb = all_state_bufs(c)
        for i in range(9):
            outs.append(P.add("sp", lambda e, i=i: e.dma_start(out=state[:, i * 4096:(i + 1) * 4096], in_=c.arena[:, i * 4096:(i + 1) * 4096]), reads=sb, dma="o"))
        outs.append(P.add("sp", lambda e: e.dma_start(out=uout, in_=c.UOUT.rearrange("p h v -> p (h v)")), reads=[c.uoutb], dma="o"))
        outs.append(P.add("sp", lambda e: e.dma_start(out=dout, in_=c.DOUT), reads=[c.uoutb], dma="o"))
        P.add("sp", None, deps=outs)
        P.emit()
    return nc


def emit_norm_mix(c, gcol0=8):
    a = c.arena
    s = c.S0
    SQ = _bf(a[:, s:s + 2048]).rearrange("p (k n) -> p k n", k=8)
    RS = a[:, s + 2048:s + 2560]
    c.P.barrier()
    emit_norm(c, gcol0, SQ, Buf(), RS, Buf())


def build_B():
    nc = bass.Bass("TRN2", target_bir_lowering=False)
    state = nc.dram_tensor("state", [128, STATE_WORDS], F32, kind="ExternalInput").ap()
    con = nc.dram_tensor("con", [128, 64], F32, kind="ExternalInput").ap()
    coef = nc.dram_tensor("coef", [128, 8], F32, kind="ExternalInput").ap()
    fix = nc.dram_tensor("fix", [128, 64], F32, kind="ExternalInput").ap()
    uall = nc.dram_tensor("uall", [128, 1536], F32, kind="ExternalInput").ap()
    dall = nc.dram_tensor("dall", [128, 12], F32, kind="ExternalInput").ap()
    halo = nc.dram_tensor("halo", [128, 64], F32, kind="ExternalInput").ap()
    w_in = nc.dram_tensor("w_in", [D, DIN], F32, kind="ExternalInput").ap()
    pool_w = nc.dram_tensor("pool_w", [4, 128, 128], F32, kind="ExternalInput").ap()
    w_pa = nc.dram_tensor("w_pa", [512, D], F32, kind="ExternalInput").ap()
    w_pb = nc.dram_tensor("w_pb", [512, D], F32, kind="ExternalInput").ap()
    w_out = nc.dram_tensor("w_out", [D, D], F32, kind="ExternalInput").ap()
    wg = nc.dram_tensor("wg", [D, DFF], F32, kind="ExternalInput").ap()
    wu = nc.dram_tensor("wu", [D, DFF], F32, kind="ExternalInput").ap()
    wd = nc.dram_tensor("wd", [DFF, D], F32, kind="ExternalInput").ap()
    xo = nc.dram_tensor("xo", [D, T], F32, kind="ExternalOutput").ap()
    fo = nc.dram_tensor("fo", [D, T], F32, kind="ExternalOutput").ap()
    with contextlib.ExitStack() as st:
        P = Prog(nc)
        c = setup_common(nc, P, st)
        load_small(c, con, coef, fix)
        sb = all_state_bufs(c)
        for i in range(9):
            P.add("sp", lambda e, i=i: e.dma_start(out=c.arena[:, i * 4096:(i + 1) * 4096], in_=state[:, i * 4096:(i + 1) * 4096]), writes=sb if i == 8 else [], dma="x")
        emit_consts(c)
        if "m2" in BSTAGES:
            emit_m2(c, w_in, pool_w, w_pa, w_pb, w_out, uall, dall, halo)
        if "ffn" in BSTAGES:
            emit_ffn(c, wg, wu, wd, 0)
        outs = []
        xov = xo.rearrange("(k p) t -> p k t", p=128)
        for k in range(8):
            outs.append(P.add("sp", lambda e, k=k: e.dma_start(out=xov[:, k, :], in_=c.X[:, k, :]), reads=[c.Xb[k][t] for t in range(NT)], dma="o"))
        outs += emit_final_norm(c, fo)
        P.add("sp", None, deps=outs)
        P.emit()
    return nc


WNAMES = [("ffn1_w_gate", [2, D, DFF]), ("ffn1_w_up", [2, D, DFF]), ("ffn1_w_down", [2, DFF, D]), ("w_in", [2, D, DIN]),
          ("pool_w", [2, 4, 128, 128]), ("w_pool_proj", [2, 512, D]), ("w_hgrn_proj", [2, 512, D]), ("w_out", [2, D, D]),
          ("ffn2_w_gate", [2, D, DFF]), ("ffn2_w_up", [2, D, DFF]), ("ffn2_w_down", [2, DFF, D])]


def build_fused():
    nc = bass.Bass("TRN2", target_bir_lowering=False)
    xT = nc.dram_tensor("xT", [D, T], F32, kind="ExternalInput").ap()
    con = nc.dram_tensor("con", [128, 128], F32, kind="ExternalInput").ap()
    coef = nc.dram_tensor("coef", [128, 8], F32, kind="ExternalInput").ap()
    fix = nc.dram_tensor("fix", [128, 64], F32, kind="ExternalInput").ap()
    W = {n: nc.dram_tensor(n, shp, F32, kind="ExternalInput").ap() for n, shp in WNAMES}
    out_d = nc.dram_tensor("out", [D, T], F32, kind="ExternalOutput").ap()
    cc_in = [nc.dram_tensor("cc_in%d" % l, [128, 580], F32, kind="Internal").ap() for l in range(2)]
    cc_out = [nc.dram_tensor("cc_out%d" % l, [512, 580], F32, kind="Internal", addr_space="Local").ap() for l in range(2)]
    with contextlib.ExitStack() as st:
        P = Prog(nc)
        c = setup_common(nc, P, st)
        cb = 48128
        c.CON = c.arena[:, cb + 416:cb + 544]
        load_small(c, con, coef, fix)
        xv = xT.rearrange("(k p) t -> p k t", p=128)
        for t in range(NT):
            P.add("sp", lambda e, t=t: e.dma_start(out=c.X[:, :, tl(t)], in_=xv[:, :, tl(t)]), writes=[c.Xb[k][t] for k in range(8)], dma="x")
        emit_consts(c)
        for l in range(2):
            base = l * 40
            emit_lb(c, 88, 92, layer=l)
            emit_ffn(c, W["ffn1_w_gate"][l], W["ffn1_w_up"][l], W["ffn1_w_down"][l], base + 0)
            ccinb, ccoutb = Buf(), Buf()

            def final_fn(h, S_ap, S_buf, l=l, ccinb=ccinb):
                P.add("sp", lambda e: e.dma_start(out=cc_in[l][:, h * 128:(h + 1) * 128], in_=S_ap), reads=[S_buf], writes=[ccinb], dma="y")
            emit_m1(c, W["w_in"][l], final_fn=final_fn, norm_gcol=base + 8)
            P.add("sp", lambda e, l=l: e.dma_start(out=cc_in[l][:, 512:516], in_=c.DOUT), reads=[c.uoutb], writes=[ccinb], dma="y")
            hw = c.arena[:, 16384:24576].rearrange("p (k w) -> p k w", k=8)[:, :, 1016:1024]
            P.add("sp", lambda e, l=l: e.dma_start(out=cc_in[l][:, 516:580].rearrange("p (k w) -> p k w", k=8), in_=hw), reads=list(c.Hb), writes=[ccinb], dma="y")
            P.add("pool", lambda e, l=l: e.collective_compute("AllGather", ALU.bypass, replica_groups=[[0, 1, 2, 3], [4, 5, 6, 7]],
                                                              ins=[cc_in[l]], outs=[cc_out[l]]),
                  reads=[ccinb], writes=[ccoutb], dma="cc")
            emit_m2(c, W["w_in"][l], W["pool_w"][l], W["w_pool_proj"][l], W["w_hgrn_proj"][l], W["w_out"][l], None, None, None,
                    ps_col=base + 24, hn_col=base + 28, cc_out=cc_out[l], ccoutb=ccoutb)
            emit_ffn(c, W["ffn2_w_gate"][l], W["ffn2_w_up"][l], W["ffn2_w_down"][l], base + 16)
        outs = emit_final_norm(c, out_d, gcol0=80)
        P.add("sp", None, deps=outs)
        P.emit()
    return nc


def _pack_con_fused(I):
    con = np.zeros((128, 128), np.float32)
    for l in range(2):
        b = l * 40
        con[:, b + 0:b + 8] = _cols(I["ffn1_norm"][l], 8)
        con[:, b + 8:b + 16] = _cols(I["mix_norm"][l], 8)
        con[:, b + 16:b + 24] = _cols(I["ffn2_norm"][l], 8)
        con[:, b + 24:b + 28] = _cols(I["pool_scale"][l], 4)
        con[:, b + 28:b + 32] = _cols(I["hgrn_norm"][l], 4)
    con[:, 80:88] = _cols(I["final_norm"], 8)
    con[:, 88:92] = _cols(I["lb_logits"][0], 4)
    con[:, 92:96] = _cols(I["lb_logits"][1], 4)
    return con


def _core_cfg_fused(core):
    pos = core % 4
    coef = np.zeros((128, 8), np.float32)
    coef[:, 3] = 1.0
    if pos >= 1:
        coef[:, pos - 1] = 1.0
        coef[:, 4 + pos - 1] = 1.0
    return coef, _core_cfg(core)[1]


def kernel(**I):
    f = lambda v: np.ascontiguousarray(np.asarray(v, dtype=np.float32))
    x = f(I["x"])
    cores = list(range(NCORES))
    con = _pack_con_fused(I)
    Wd = {n: f(I[n]) for n, _ in WNAMES}
    in_maps = []
    for i in cores:
        coef, fix = _core_cfg_fused(i)
        xs = np.ascontiguousarray(x[i // 4, (i % 4) * T:(i % 4 + 1) * T, :].T)
        in_maps.append(dict(xT=xs, con=con, coef=coef, fix=fix, **Wd))
    if "F" not in _PROGS:
        _PROGS["F"] = build_fused()
    res = run_bass_kernel_spmd(_PROGS["F"], in_maps, core_ids=cores).results
    out = np.empty((2, 8192, D), np.float32)
    for i in cores:
        out[i // 4, (i % 4) * T:(i % 4 + 1) * T, :] = res[i]["out"].T
    return out


def _cols(v, n):
    return np.ascontiguousarray(np.asarray(v, np.float32).reshape(n, 128).T)


def _pack_con(ffn_norm, mix_norm, final_norm, pool_scale, hgrn_norm, lb0, lb1, lb_on):
    con = np.zeros((128, 64), np.float32)
    con[:, 0:8] = _cols(ffn_norm, 8)
    con[:, 8:16] = _cols(mix_norm, 8)
    con[:, 16:24] = _cols(final_norm, 8)
    con[:, 24:28] = _cols(pool_scale, 4)
    con[:, 28:32] = _cols(hgrn_norm, 4)
    con[:, 32:36] = _cols(lb0, 4)
    con[:, 36:40] = _cols(lb1, 4)
    con[:, 40] = lb_on
    return con


def _core_cfg(core):
    pos = core % 4
    coef = np.zeros((128, 8), np.float32)
    if pos >= 1:
        coef[:, pos - 1] = 1.0
        coef[:, 3] = 1.0
    fix = np.ones((128, 4, 16), np.float32)
    if pos == 0:
        for g in range(4):
            w = 2 ** (g + 1)
            for t in range(16):
                fix[:, g, t] = np.float32(w) / np.float32(min(t + 1, w))
    return coef, fix.reshape(128, 64)


_PROGS = {}
_DBG = None


def _prog(name):
    if name not in _PROGS:
        _PROGS[name] = {"A": build_A, "B": build_B}[name]()
    return _PROGS[name]


def kernel_unfused(x, ffn1_norm, ffn1_w_gate, ffn1_w_up, ffn1_w_down, mix_norm, w_in, pool_w,
           pool_scale, lb_logits, hgrn_norm, w_pool_proj, w_hgrn_proj, w_out,
           ffn2_norm, ffn2_w_gate, ffn2_w_up, ffn2_w_down, final_norm):
    f = lambda v: np.ascontiguousarray(np.asarray(v, dtype=np.float32))
    x = f(x)
    cores = list(range(NCORES))
    xs = [np.ascontiguousarray(x[cidx // 4, (cidx % 4) * T:(cidx % 4 + 1) * T, :].T) for cidx in cores]
    cfg = [_core_cfg(cidx) for cidx in cores]
    L = 2
    final = None
    for l in range(L):
        conA = _pack_con(ffn1_norm[l], mix_norm[l], final_norm, pool_scale[l], hgrn_norm[l], lb_logits[0], lb_logits[1], 0.0 if l == 0 else 1.0)
        conB = _pack_con(ffn2_norm[l], mix_norm[l], final_norm, pool_scale[l], hgrn_norm[l], lb_logits[0], lb_logits[1], 0.0 if l == 0 else 1.0)
        wA = {"wg": f(ffn1_w_gate[l]), "wu": f(ffn1_w_up[l]), "wd": f(ffn1_w_down[l]), "w_in": f(w_in[l])}
        inA = [dict(xT=xs[i], con=conA, coef=cfg[i][0], fix=cfg[i][1], **wA) for i in cores]
        rA = run_bass_kernel_spmd(_prog("A"), inA, core_ids=cores).results
        inB = []
        wB = {"w_in": wA["w_in"], "pool_w": f(pool_w[l]), "w_pa": f(w_pool_proj[l]), "w_pb": f(w_hgrn_proj[l]), "w_out": f(w_out[l]),
              "wg": f(ffn2_w_gate[l]), "wu": f(ffn2_w_up[l]), "wd": f(ffn2_w_down[l])}
        for i in cores:
            g0 = (i // 4) * 4
            uall = np.ascontiguousarray(np.concatenate([rA[g0 + j]["uout"] for j in range(3)], axis=1))
            dall = np.ascontiguousarray(np.concatenate([rA[g0 + j]["dout"] for j in range(3)], axis=1))
            prev = rA[(i - 1) % NCORES]["state"]
            hreg = prev[:, 16384:24576].reshape(128, 8, 1024)
            halo = np.ascontiguousarray(hreg[:, :, 1016:1024].reshape(128, 64))
            inB.append(dict(state=rA[i]["state"], con=conB, coef=cfg[i][0], fix=cfg[i][1], uall=uall, dall=dall, halo=halo, **wB))
        rB = run_bass_kernel_spmd(_prog("B"), inB, core_ids=cores).results
        xs = [rB[i]["xo"] for i in cores]
        final = [rB[i]["fo"] for i in cores]
        if _DBG is not None:
            _DBG["x_l%d" % l] = xs
    out = np.empty((2, 8192, D), np.float32)
    for i in cores:
        out[i // 4, (i % 4) * T:(i % 4 + 1) * T, :] = final[i].T
    return out
```
